# Optimizing a Trainium2 kernel written in Bass

```python
import math
import jax, jax.numpy as jnp
from jax import lax
import numpy as np

D_MODEL = 1024
BATCH = 8
SEQ = 2048
DEPTH = 2

CTX_LEN = 256
GRID_W = 64
D_MIX = D_MODEL
GROUP_WIDTH = D_MIX // 4
HEAD_DIM = 64
GQA_HEADS = GROUP_WIDTH // HEAD_DIM
GQA_KV_HEADS = 2
GQA_GROUP = GQA_HEADS // GQA_KV_HEADS
DIFF_HEADS = GROUP_WIDTH // HEAD_DIM
DIFF_QK_DIM = HEAD_DIM // 2
POOL_WINDOWS = (2, 4, 8, 16)
POOL_GROUP_DIM = GROUP_WIDTH // len(POOL_WINDOWS)
RET_HEADS = GROUP_WIDTH // HEAD_DIM
RET_CHUNK = 128
D_FF = 2816
FFN_RESIDUAL = 0.5
Q_BLOCK = 128
ROPE_BASE = 10000.0
EPS = 1e-6
IN_SPLITS = (GQA_HEADS * HEAD_DIM, GQA_KV_HEADS * HEAD_DIM, GQA_KV_HEADS * HEAD_DIM,
             DIFF_HEADS * HEAD_DIM, DIFF_HEADS * HEAD_DIM, DIFF_HEADS * HEAD_DIM,
             GROUP_WIDTH,
             RET_HEADS * HEAD_DIM, RET_HEADS * HEAD_DIM, RET_HEADS * HEAD_DIM, GROUP_WIDTH)
D_IN = sum(IN_SPLITS)
IN_SPLIT_POINTS = tuple(int(s) for s in np.cumsum(IN_SPLITS)[:-1])

kernel_name = "hybrid_parallel_head_dit_block"

F32 = jnp.float32


def rms_norm(x, g):
    xf = x.astype(F32)
    y = xf * lax.rsqrt(jnp.mean(xf * xf, axis=-1, keepdims=True) + EPS)
    return (y * g.astype(F32)).astype(x.dtype)


def modulate(x, g, shift, scale):
    return rms_norm(x, g) * (1 + scale) + shift


def swiglu(h, wg, wu, wd):
    return (jax.nn.silu(h @ wg) * (h @ wu)) @ wd


def split_heads(t, n):
    b, l, _ = t.shape
    return t.reshape(b, l, n, -1).transpose(0, 2, 1, 3)


def merge_heads(t):
    b, n, l, d = t.shape
    return t.transpose(0, 2, 1, 3).reshape(b, l, n * d)


def axial_rope_tables(n, dim):
    n_rows = n // GRID_W
    rows = jnp.repeat(jnp.arange(n_rows), GRID_W).astype(F32)
    cols = jnp.tile(jnp.arange(GRID_W), n_rows).astype(F32)
    quarter = dim // 4
    inv = ROPE_BASE ** (-jnp.arange(quarter, dtype=F32) / quarter)
    ang = jnp.concatenate([rows[:, None] * inv, cols[:, None] * inv], axis=-1)
    return jnp.cos(ang), jnp.sin(ang)


def apply_axial_rope(x, cos, sin):
    shp = x.shape
    q = shp[-1] // 4
    xr = x.astype(F32).reshape(shp[:-1] + (2, 2, q))
    x1, x2 = xr[..., 0, :], xr[..., 1, :]
    c = cos.reshape(-1, 2, q)
    s = sin.reshape(-1, 2, q)
    out = jnp.stack([x1 * c - x2 * s, x2 * c + x1 * s], axis=-2)
    return out.reshape(shp).astype(x.dtype)


def blocked_attention(qs, ks, coefs, v, scale):
    b, hk, g, l, _ = qs[0].shape
    nb = l // Q_BLOCK
    qb = tuple(jnp.moveaxis(q.reshape(b, hk, g, nb, Q_BLOCK, q.shape[-1]), 3, 0) for q in qs)

    def one_block(qblk):
        acc = 0.0
        for qi, ki, ci in zip(qblk, ks, coefs):
            s = jnp.einsum('bkgqd,bksd->bkgqs', qi, ki, preferred_element_type=F32) * scale
            acc = acc + ci * jax.nn.softmax(s, axis=-1)
        return jnp.einsum('bkgqs,bkse->bkgqe', acc.astype(v.dtype), v)

    out = lax.map(one_block, qb)
    out = jnp.moveaxis(out, 0, 3).reshape(b, hk * g, l, v.shape[-1])
    return merge_heads(out)


def retention_chunkwise(q, k, v, log_g, s0):
    b, h, l, dk = q.shape
    dv = v.shape[-1]
    c = RET_CHUNK
    nc = l // c
    qc = q.reshape(b, h, nc, c, dk)
    kc = k.reshape(b, h, nc, c, dk)
    vc = v.reshape(b, h, nc, c, dv)
    pos = jnp.arange(c, dtype=F32)
    lg = log_g[:, None]
    diff = pos[:, None] - pos[None, :]
    decay_intra = jnp.where(diff >= 0, jnp.exp(jnp.maximum(diff, 0.0) * log_g[:, None, None]), 0.0)
    scores = jnp.einsum('bhncd,bhnsd->bhncs', qc, kc) * decay_intra[:, None]
    intra = jnp.einsum('bhncs,bhnse->bhnce', scores, vc)
    k_dec = kc * jnp.exp((c - 1 - pos) * lg)[:, None, :, None]
    kv = jnp.einsum('bhncd,bhnce->nbhde', k_dec, vc)
    chunk_decay = jnp.exp(c * log_g)[:, None, None]

    def step(s, kv_n):
        return s * chunk_decay + kv_n, s

    _, s_before = lax.scan(step, s0, kv)
    q_dec = qc * jnp.exp((pos + 1) * lg)[:, None, :, None]
    cross = jnp.einsum('bhncd,nbhde->bhnce', q_dec, s_before)
    return (intra + cross).reshape(b, h, l, dv)


def retention_state(k, v, log_g):
    l = k.shape[2]
    w = jnp.exp((l - 1 - jnp.arange(l, dtype=F32))[None, :] * log_g[:, None])
    return jnp.einsum('bhld,bhle,hl->bhde', k, v, w)


def flip_seq(t):
    return jnp.flip(t, axis=2)


def bidir_retention(q, k, v, log_g2, s_f, s_b):
    fwd = retention_chunkwise(q, k, v, log_g2[0], s_f)
    bwd = flip_seq(retention_chunkwise(flip_seq(q), flip_seq(k), flip_seq(v), log_g2[1], s_b))
    return fwd + bwd


def multiscale_pool(u, pool_w, pool_scale):
    b, l, _ = u.shape
    ng = len(POOL_WINDOWS)
    uf = u.astype(F32).reshape(b, l, ng, POOL_GROUP_DIM)
    cs = jnp.pad(jnp.cumsum(uf, axis=1), ((0, 0), (1, 0), (0, 0), (0, 0)))
    t = jnp.arange(l)
    outs = []
    for gi, w in enumerate(POOL_WINDOWS):
        back = w // 2
        ahead = w - 1 - back
        csg = jnp.pad(cs[:, :, gi], ((0, 0), (back, ahead), (0, 0)), mode='edge')
        wsum = csg[:, w:w + l] - csg[:, :l]
        count = (jnp.minimum(t + ahead, l - 1) - jnp.maximum(t - back, 0) + 1).astype(F32)
        outs.append(wsum / count[None, :, None] - uf[:, :, gi])
    pooled = jnp.stack(outs, axis=2)
    y = jnp.einsum('blgc,gcd->blgd', pooled, pool_w.astype(F32))
    return (y.reshape(b, l, -1) * pool_scale.astype(F32)).astype(u.dtype)


def gqa_group(ctx_qkv, lat_qkv, qk_gain, rope, with_ctx):
    def prep(q, k, v):
        q = rms_norm(split_heads(q, GQA_HEADS), qk_gain[0])
        k = rms_norm(split_heads(k, GQA_KV_HEADS), qk_gain[1])
        return q, k, split_heads(v, GQA_KV_HEADS)

    cq, ck, cv = prep(*ctx_qkv)
    lq, lk, lv = prep(*lat_qkv)
    lq = apply_axial_rope(lq, *rope)
    lk = apply_axial_rope(lk, *rope)

    def grp(q):
        return q.reshape(q.shape[0], GQA_KV_HEADS, GQA_GROUP, q.shape[2], HEAD_DIM)

    scale = HEAD_DIM ** -0.5
    y_lat = blocked_attention((grp(lq),), (jnp.concatenate([ck, lk], axis=2),), (1.0,),
                              jnp.concatenate([cv, lv], axis=2), scale)
    y_ctx = blocked_attention((grp(cq),), (ck,), (1.0,), cv, scale) if with_ctx else None
    return y_ctx, y_lat


def diff_group(ctx_qkv, lat_qkv, lam_params, out_gain, rope, layer_idx, with_ctx):
    lam_init = 0.8 - 0.6 * math.exp(-0.3 * layer_idx)
    lp = lam_params.astype(F32)
    lam = jnp.exp(jnp.sum(lp[0] * lp[1])) - jnp.exp(jnp.sum(lp[2] * lp[3])) + lam_init

    def prep(q, k, v):
        q = split_heads(q, DIFF_HEADS)
        k = split_heads(k, DIFF_HEADS)
        return (q[..., :DIFF_QK_DIM], q[..., DIFF_QK_DIM:], k[..., :DIFF_QK_DIM], k[..., DIFF_QK_DIM:],
                split_heads(v, DIFF_HEADS))

    cq1, cq2, ck1, ck2, cv = prep(*ctx_qkv)
    lq1, lq2, lk1, lk2, lv = prep(*lat_qkv)
    lq1, lq2, lk1, lk2 = (apply_axial_rope(t, *rope) for t in (lq1, lq2, lk1, lk2))
    scale = DIFF_QK_DIM ** -0.5
    coefs = (1.0, -lam)

    def finish(y):
        b, l, _ = y.shape
        y = rms_norm(y.reshape(b, l, DIFF_HEADS, HEAD_DIM), out_gain) * (1.0 - lam_init)
        return y.reshape(b, l, -1)

    def cat(a, b):
        return jnp.concatenate([a, b], axis=2)

    y_lat = finish(blocked_attention((lq1[:, :, None], lq2[:, :, None]), (cat(ck1, lk1), cat(ck2, lk2)),
                                     coefs, cat(cv, lv), scale))
    y_ctx = finish(blocked_attention((cq1[:, :, None], cq2[:, :, None]), (ck1, ck2), coefs, cv, scale)) \
        if with_ctx else None
    return y_ctx, y_lat


def retention_group(ctx_qkvg, lat_qkvg, decay_logit, out_gain, with_ctx):
    log_g = jax.nn.log_sigmoid(decay_logit.astype(F32))

    def prep(q, k, v):
        f = lambda t: split_heads(t, RET_HEADS).astype(F32)
        return f(q), f(k) * HEAD_DIM ** -0.5, f(v)

    cq, ck, cv = prep(*ctx_qkvg[:3])
    lq, lk, lv = prep(*lat_qkvg[:3])
    s_f = retention_state(ck, cv, log_g[0])
    s_b = retention_state(flip_seq(ck), flip_seq(cv), log_g[1])

    def finish(o, g):
        o = rms_norm(o, out_gain)
        return (merge_heads(o) * jax.nn.silu(g.astype(F32))).astype(g.dtype)

    y_lat = finish(bidir_retention(lq, lk, lv, log_g, s_f, s_b), lat_qkvg[3])
    y_ctx = None
    if with_ctx:
        zeros = jnp.zeros_like(s_f)
        y_ctx = finish(bidir_retention(cq, ck, cv, log_g, zeros, zeros), ctx_qkvg[3])
    return y_ctx, y_lat


def token_mixers(u_ctx, u_lat, rope_a, rope_b, qk_gain, lam_params, diff_gain, pool_w, pool_scale,
                 decay_logit, ret_gain, layer_idx, with_ctx):
    c_parts = jnp.split(u_ctx, IN_SPLIT_POINTS, axis=-1)
    l_parts = jnp.split(u_lat, IN_SPLIT_POINTS, axis=-1)
    a_ctx, a_lat = gqa_group(c_parts[0:3], l_parts[0:3], qk_gain, rope_a, with_ctx)
    b_ctx, b_lat = diff_group(c_parts[3:6], l_parts[3:6], lam_params, diff_gain, rope_b, layer_idx, with_ctx)
    c_lat = multiscale_pool(l_parts[6], pool_w, pool_scale)
    d_ctx, d_lat = retention_group(c_parts[7:11], l_parts[7:11], decay_logit, ret_gain, with_ctx)
    y_lat = jnp.concatenate([a_lat, b_lat, c_lat, d_lat], axis=-1)
    y_ctx = None
    if with_ctx:
        c_ctx_pool = multiscale_pool(c_parts[6], pool_w, pool_scale)
        y_ctx = jnp.concatenate([a_ctx, b_ctx, c_ctx_pool, d_ctx], axis=-1)
    return y_ctx, y_lat


def ffn_sublayer(x, m, sub, g_pre, g_post, wg, wu, wd):
    h = modulate(x, g_pre, m[:, :, sub, 0], m[:, :, sub, 1])
    return x + FFN_RESIDUAL * m[:, :, sub, 2] * rms_norm(swiglu(h, wg, wu, wd), g_post)


def setup_inputs(seed: int = 0) -> dict:
    key = jax.random.key(seed)
    ks = jax.random.split(key, 20)
    nrm = jax.random.normal
    h = jnp.arange(RET_HEADS, dtype=F32)
    gamma = 1.0 - 2.0 ** (-5.0 - h)
    decay_logit0 = jnp.log(gamma) - jnp.log1p(-gamma)
    return {
        "x": nrm(ks[0], (BATCH, SEQ, D_MODEL), F32),
        "c": nrm(ks[1], (BATCH, D_MODEL), F32),
        "ctx": nrm(ks[2], (BATCH, CTX_LEN, D_MODEL), F32),
        "c_ctx": nrm(ks[3], (D_MODEL,), F32),
        "w_mod": nrm(ks[4], (DEPTH, D_MODEL, 9 * D_MODEL), F32) * (0.3 * D_MODEL ** -0.5),
        "b_mod": nrm(ks[5], (DEPTH, 9 * D_MODEL), F32) * 0.02,
        "norm_gain": 1.0 + 0.02 * nrm(ks[6], (DEPTH, 6, D_MODEL), F32),
        "ffn_w_gate": nrm(ks[7], (DEPTH, 2, D_MODEL, D_FF), F32) * D_MODEL ** -0.5,
        "ffn_w_up": nrm(ks[8], (DEPTH, 2, D_MODEL, D_FF), F32) * D_MODEL ** -0.5,
        "ffn_w_down": nrm(ks[9], (DEPTH, 2, D_FF, D_MODEL), F32) * D_FF ** -0.5,
        "w_in": nrm(ks[10], (DEPTH, D_MODEL, D_IN), F32) * D_MODEL ** -0.5,
        "w_out": nrm(ks[11], (DEPTH, D_MIX, D_MODEL), F32) * D_MIX ** -0.5,
        "attn_qk_gain": 1.0 + 0.02 * nrm(ks[12], (DEPTH, 2, HEAD_DIM), F32),
        "diff_lambda": 0.1 * nrm(ks[13], (DEPTH, 4, DIFF_QK_DIM), F32),
        "diff_out_gain": 1.0 + 0.02 * nrm(ks[14], (DEPTH, HEAD_DIM), F32),
        "pool_w": nrm(ks[15], (DEPTH, len(POOL_WINDOWS), POOL_GROUP_DIM, POOL_GROUP_DIM), F32) * POOL_GROUP_DIM ** -0.5,
        "pool_scale": 1.0 + 0.02 * nrm(ks[16], (DEPTH, GROUP_WIDTH), F32),
        "ret_decay_logit": decay_logit0 + 0.1 * nrm(ks[17], (DEPTH, 2, RET_HEADS), F32),
        "ret_out_gain": 1.0 + 0.02 * nrm(ks[18], (DEPTH, HEAD_DIM), F32),
    }


def reference(x, c, ctx, c_ctx, w_mod, b_mod, norm_gain, ffn_w_gate, ffn_w_up, ffn_w_down, w_in, w_out,
              attn_qk_gain, diff_lambda, diff_out_gain, pool_w, pool_scale, ret_decay_logit, ret_out_gain):
    b, n, _ = x.shape
    rope_a = axial_rope_tables(n, HEAD_DIM)
    rope_b = axial_rope_tables(n, DIFF_QK_DIM)
    silu_c = jax.nn.silu(c)
    silu_cc = jax.nn.silu(c_ctx)
    h_lat, h_ctx = x, ctx
    for i in range(DEPTH):
        last = i == DEPTH - 1
        m_lat = (silu_c @ w_mod[i] + b_mod[i]).reshape(b, 1, 3, 3, D_MODEL).astype(x.dtype)
        m_ctx = (silu_cc @ w_mod[i] + b_mod[i]).reshape(1, 1, 3, 3, D_MODEL).astype(x.dtype)
        g = norm_gain[i]
        h_ctx = ffn_sublayer(h_ctx, m_ctx, 0, g[0], g[1], ffn_w_gate[i, 0], ffn_w_up[i, 0], ffn_w_down[i, 0])
        h_lat = ffn_sublayer(h_lat, m_lat, 0, g[0], g[1], ffn_w_gate[i, 0], ffn_w_up[i, 0], ffn_w_down[i, 0])
        u_ctx = modulate(h_ctx, g[2], m_ctx[:, :, 1, 0], m_ctx[:, :, 1, 1]) @ w_in[i]
        u_lat = modulate(h_lat, g[2], m_lat[:, :, 1, 0], m_lat[:, :, 1, 1]) @ w_in[i]
        y_ctx, y_lat = token_mixers(u_ctx, u_lat, rope_a, rope_b, attn_qk_gain[i], diff_lambda[i],
                                    diff_out_gain[i], pool_w[i], pool_scale[i], ret_decay_logit[i],
                                    ret_out_gain[i], i, not last)
        h_lat = h_lat + m_lat[:, :, 1, 2] * rms_norm(y_lat @ w_out[i], g[3])
        h_lat = ffn_sublayer(h_lat, m_lat, 2, g[4], g[5], ffn_w_gate[i, 1], ffn_w_up[i, 1], ffn_w_down[i, 1])
        if not last:
            h_ctx = h_ctx + m_ctx[:, :, 1, 2] * rms_norm(y_ctx @ w_out[i], g[3])
            h_ctx = ffn_sublayer(h_ctx, m_ctx, 2, g[4], g[5], ffn_w_gate[i, 1], ffn_w_up[i, 1], ffn_w_down[i, 1])
    return h_lat
```

```python
import math
import numpy as np
from contextlib import ExitStack
import concourse.bass as bass
import concourse.mybir as mybir
from concourse.bass_utils import run_bass_kernel_spmd

F32 = mybir.dt.float32
BF16 = mybir.dt.bfloat16
AF = mybir.ActivationFunctionType
ALU = mybir.AluOpType

class _Rec:
    def __getattr__(self, name):
        def mk(*a, **kw):
            return lambda e: getattr(e, name)(*a, **kw)
        return mk


I = _Rec()

D = 1024
NLAT = 2048
NCTX = 256
NTOK = NLAT + NCTX
DFF = 2816
NFC = DFF // 128
DEPTH = 2
EPS = 1e-6


class Buf:
    __slots__ = ("name", "writers", "readers", "sem", "ntick", "last_dma")

    def __init__(self, name):
        self.name = name
        self.writers = {}
        self.readers = {}
        self.sem = None
        self.ntick = 0
        self.last_dma = None


class Ins:
    __slots__ = ("eng", "fn", "deps", "idx", "sig", "tick", "is_dma", "sembuf", "dtick", "where")


class KB:
    ENG = ("pe", "act", "dve", "pool", "sp")

    def __init__(self):
        self.nc = bass.Bass("TRN2", target_bir_lowering=False)
        self.es = ExitStack()
        self.q = {e: [] for e in self.ENG}
        self.bufs = {}
        self.banks = []
        self.nsb = 0
        self.fence = {}
        self.where = {}

    def buf(self, *key):
        b = self.bufs.get(key)
        if b is None:
            b = self.bufs[key] = Buf(key)
        return b

    def sbuf(self, name, shape, dt):
        return self.es.enter_context(self.nc.sbuf_tensor("sb_" + name, list(shape), dt))

    def psum_banks(self):
        for i in range(8):
            t = self.es.enter_context(self.nc.psum_tensor(f"bank{i}", [128, 512], F32))
            self.banks.append((t, self.buf("bank", i)))

    def dram(self, name, shape, dt, kind):
        return self.nc.dram_tensor(name, list(shape), dt, kind=kind).ap()

    def _add(self, eng, fn, r, w, is_dma=False, sembuf=None):
        ins = Ins()
        ins.eng = eng
        ins.fn = fn
        ins.is_dma = is_dma
        ins.sig = False
        ins.tick = 0
        ins.sembuf = sembuf
        ins.dtick = 0
        ins.idx = len(self.q[eng])
        import sys as _sys
        f_ = _sys._getframe(2)
        ins.where = f"{f_.f_code.co_name}:{f_.f_lineno}"
        deps = set()
        for b in r:
            deps.update(b.writers.values())
            if b.name[0] == "bank":
                deps.update(v for kk, v in b.readers.items() if kk != eng)
        for b in w:
            deps.update(b.writers.values())
            deps.update(b.readers.values())
        if self.fence.get(eng):
            deps.update(self.fence.pop(eng))
        if is_dma:
            if sembuf.last_dma is not None:
                deps.add(sembuf.last_dma)
            sembuf.last_dma = ins
            sembuf.ntick += 1
            ins.dtick = 16 * sembuf.ntick
        ins.deps = deps
        key = ("dma", id(sembuf)) if is_dma else eng
        for b in w:
            if b.readers:
                b.writers = {key: ins}
                b.readers = {}
            else:
                b.writers[key] = ins
        for b in r:
            b.readers[key] = ins
        self.q[eng].append(ins)
        return ins

    def barrier(self):
        fence = set()
        for e in self.ENG:
            if self.q[e]:
                fence.add(self.q[e][-1])
        for b in self.bufs.values():
            if b.last_dma is not None:
                fence.add(b.last_dma)
        self.fence = {e: set(fence) for e in self.ENG}

    def op(self, eng, fn, r=(), w=()):
        return self._add(eng, fn, list(r), list(w))

    def dma(self, queue, out, in_, r=(), w=(), sembuf=None):
        r = list(r)
        w = list(w)
        if sembuf is None:
            sembuf = w[0] if w else r[0]
        return self._add(queue, I.dma_start(out=out, in_=in_), r, w, True, sembuf)

    def _needs_wait(self, ins, d):
        if d.is_dma:
            return True
        if d.eng != ins.eng:
            return True
        if ins.is_dma:
            return True
        if ins.eng == "pe":
            return False
        return ins.idx - d.idx <= 2

    def emit(self):
        nc = self.nc
        es = self.es
        for e in self.ENG:
            for ins in self.q[e]:
                for d in ins.deps:
                    if (not d.is_dma) and self._needs_wait(ins, d):
                        d.sig = True
        for e in self.ENG:
            t = 0
            for ins in self.q[e]:
                if ins.sig and not ins.is_dma:
                    t += 1
                ins.tick = t
        engsem = {e: es.enter_context(nc.semaphore(f"sem_{e}")) for e in ("pe", "act", "dve", "pool")}
        nsem = 4
        for b in self.bufs.values():
            if b.ntick > 0:
                b.sem = es.enter_context(nc.semaphore(f"dsem{nsem}"))
                nsem += 1
        self.nsem = nsem
        block = es.enter_context(nc.Block())

        def run(ename, e):
            waited = {}
            mydma = {}
            for ins in self.q[ename]:
                need = {}
                for d in ins.deps:
                    if not self._needs_wait(ins, d):
                        continue
                    if d.is_dma:
                        key = ("dma", id(d.sembuf))
                        sem = d.sembuf.sem
                        val = d.dtick
                    else:
                        key = d.eng
                        sem = engsem[d.eng]
                        val = d.tick
                    if key not in need or need[key][1] < val:
                        need[key] = (sem, val)
                for key, (sem, val) in need.items():
                    if waited.get(key, 0) < val:
                        e.wait_ge(sem, val)
                        waited[key] = val
                bi = ins.fn(e)
                try:
                    self.where[str(bi.ins.name)] = ins.where
                except Exception:
                    try:
                        self.where[str(bi.name)] = ins.where
                    except Exception:
                        pass
                if ins.is_dma:
                    bi.then_inc(ins.sembuf.sem, 16)
                    mydma[id(ins.sembuf)] = (ins.sembuf.sem, ins.dtick)
                elif ins.sig:
                    bi.then_inc(engsem[ename], 1)
            for sem, val in mydma.values():
                e.wait_ge(sem, val)

        @block.sync
        def _(e):
            run("sp", e)

        @block.tensor
        def _(e):
            run("pe", e)

        @block.scalar
        def _(e):
            run("act", e)

        @block.vector
        def _(e):
            run("dve", e)

        @block.gpsimd
        def _(e):
            run("pool", e)

        es.close()
        return nc


def _colblocks(w, ncols=128):
    K, N = w.shape
    kc = K // 128
    nb = N // ncols
    return np.ascontiguousarray(w.reshape(kc, 128, nb, ncols).transpose(2, 1, 0, 3).reshape(nb, 128, kc * ncols))


def _pp(v):
    n = v.shape[0] // 128
    return np.ascontiguousarray(v.reshape(n, 128).T)


def _r(a, b):
    return list(range(a, b))


FM_CHUNKS = [
    ("AK", _r(256, 384)), ("BK0", _r(768, 896)), ("BK1", _r(896, 1024)), ("C0", _r(1280, 1408)), ("C1", _r(1408, 1536)),
    ("DK0", _r(1792, 1920)), ("DK1", _r(1920, 2048)),
    ("AQ0", _r(0, 64) + _r(128, 192)), ("AQ1", _r(64, 128) + _r(192, 256)), ("BQ0", _r(512, 640)), ("BQ1", _r(640, 768)),
    ("DQ0", _r(1536, 1664)), ("DQ1", _r(1664, 1792)), ("G0", _r(2304, 2432)), ("G1", _r(2432, 2560)),
]
FM_IDX = {n: i for i, (n, _) in enumerate(FM_CHUNKS)}
TM_COLS = _r(384, 512) + _r(1024, 1280) + _r(1792, 2048) + _r(2048, 2304)
WOUT_ROWS = _r(0, 64) + _r(128, 192) + _r(64, 128) + _r(192, 256) + _r(256, 1024)
POOL_WINDOWS = (2, 4, 8, 16)


def host_consts():
    c = {}
    n = NLAT
    rows = np.repeat(np.arange(n // 64), 64).astype(np.float32)
    cols = np.tile(np.arange(64), n // 64).astype(np.float32)

    def tables(dim):
        quarter = dim // 4
        inv = (np.float32(10000.0) ** (-np.arange(quarter, dtype=np.float32) / np.float32(quarter))).astype(np.float32)
        ang = np.concatenate([rows[:, None] * inv, cols[:, None] * inv], axis=-1).astype(np.float32)
        return np.cos(ang).astype(np.float32), np.sin(ang).astype(np.float32)

    cosA, sinA = tables(64)
    cosB, sinB = tables(32)
    p = np.arange(128)
    dA = p % 64
    idxA = (dA // 32) * 16 + (dA % 16)
    halfA = (dA % 32) // 16
    dB = p % 32
    idxB = (dB // 16) * 8 + (dB % 8)
    halfB = (dB % 16) // 8
    c["ropeA"] = np.ascontiguousarray(np.stack([cosA[:, idxA].T, sinA[:, idxA].T], axis=0))
    c["ropeB"] = np.ascontiguousarray(np.stack([cosB[:, idxB].T, sinB[:, idxB].T], axis=0))
    RA = np.zeros((128, 128), np.float32)
    RB = np.zeros((128, 128), np.float32)
    for m in range(128):
        pa = m + 16 if halfA[m] == 0 else m - 16
        RA[pa, m] = -1.0 if halfA[m] == 0 else 1.0
        pb = m + 8 if halfB[m] == 0 else m - 8
        RB[pb, m] = -1.0 if halfB[m] == 0 else 1.0
    c["perm"] = np.ascontiguousarray(np.stack([RA, RB], axis=1))
    mm = p[:, None].astype(np.float32)
    nn = p[None, :].astype(np.float32)
    cst = np.stack([np.maximum(nn - mm, 0), np.maximum(mm - nn, 0), np.eye(128, dtype=np.float32),
                    np.broadcast_to(nn + 1, (128, 128)), np.broadcast_to(128 - nn, (128, 128))], axis=1)
    c["cst"] = np.ascontiguousarray(cst.astype(np.float32))
    c["ccol"] = np.ascontiguousarray(np.stack([127 - p, p], axis=1).astype(np.float32))

    def invc(l):
        t = np.arange(l)
        out = np.zeros((128, 2, l), np.float32)
        for g, w in enumerate(POOL_WINDOWS):
            back = w // 2
            ahead = w - 1 - back
            cnt = (np.minimum(t + ahead, l - 1) - np.maximum(t - back, 0) + 1).astype(np.float32)
            ch, half = g // 2, g % 2
            out[half * 64:(half + 1) * 64, ch, :] = (np.float32(1.0) / cnt)[None, :]
        return out

    c["invc_lat"] = invc(NLAT)
    c["invc_ctx"] = invc(NCTX)
    return c


def host_prep(inputs):
    sh = dict(host_consts())
    w_mod = inputs["w_mod"]
    for i in range(DEPTH):
        sh[f"wm{i}"] = _colblocks(w_mod[i])
        bm = _pp(inputs["b_mod"][i])
        sh[f"bm{i}"] = np.ascontiguousarray(np.repeat(bm[:, :, None], 2, axis=2))
        ng = inputs["norm_gain"][i]
        ngp = np.stack([_pp(ng[j]) for j in range(6)], axis=1)
        sh[f"ng{i}"] = np.ascontiguousarray(np.repeat(ngp[:, :, :, None], 2, axis=3))
        for s in range(2):
            sh[f"wg{i}{s}"] = _colblocks(inputs["ffn_w_gate"][i, s])
            sh[f"wu{i}{s}"] = _colblocks(inputs["ffn_w_up"][i, s])
            sh[f"wd{i}{s}"] = _colblocks(inputs["ffn_w_down"][i, s])
        w_in = inputs["w_in"][i]
        sh[f"winf{i}"] = np.ascontiguousarray(np.stack([_colblocks(w_in[:, cols])[0] for _, cols in FM_CHUNKS], axis=0))
        wt = w_in[:, TM_COLS].reshape(8, 128, 896).transpose(1, 0, 2).reshape(128, 7, 1024)
        sh[f"wt{i}"] = np.ascontiguousarray(wt.transpose(1, 0, 2))
        sh[f"wo{i}"] = _colblocks(inputs["w_out"][i][WOUT_ROWS, :])
        pv = np.zeros((128, 8), np.float32)
        pv[:, 0] = np.tile(inputs["attn_qk_gain"][i, 0], 2)
        pv[:, 1] = np.tile(inputs["attn_qk_gain"][i, 1], 2)
        pv[:, 2] = np.tile(inputs["diff_out_gain"][i], 2)
        pv[:, 3] = np.tile(inputs["ret_out_gain"][i], 2)
        pv[:, 4:6] = _pp(inputs["pool_scale"][i])
        sh[f"pv{i}"] = pv
        pw = np.zeros((128, 2, 128), np.float32)
        for g in range(4):
            ch, half = g // 2, g % 2
            pw[half * 64:(half + 1) * 64, ch, half * 64:(half + 1) * 64] = inputs["pool_w"][i, g]
        sh[f"pw{i}"] = pw
        sh[f"rdl{i}"] = np.ascontiguousarray(np.broadcast_to(inputs["ret_decay_logit"][i].reshape(1, 8), (128, 8)))
        sh[f"dl{i}"] = np.ascontiguousarray(np.broadcast_to(inputs["diff_lambda"][i].reshape(1, 128), (128, 128)))
    per = []
    B = inputs["x"].shape[0]
    for b in range(B):
        d = {}
        xt = np.concatenate([inputs["ctx"][b].T, inputs["x"][b].T], axis=1)
        d["xT"] = np.ascontiguousarray(xt.reshape(8, 128, NTOK))
        cc = np.stack([inputs["c"][b], inputs["c_ctx"]], axis=1)
        d["cT"] = np.ascontiguousarray(cc.reshape(8, 128, 2).transpose(1, 0, 2))
        per.append(d)
    return sh, per


LAT, CTX = 0, 1
SCB = (0, 1, 2, 3, 6, 7)
NT = NTOK // 128


class _Stop(Exception):
    pass


class Prog:
    def chk(self, tag):
        if getattr(self, "kv_stop", None) == tag:
            raise _Stop()

    def __init__(self, shapes):
        self.k = KB()
        k = self.k

        class _Lazy(dict):
            def __missing__(d, name):
                shape, dt = shapes[name]
                d[name] = k.dram(name, shape, dt, "ExternalInput")
                return d[name]

        self.inp = _Lazy()
        self.out = k.dram("out", [8, 128, NLAT], F32, "ExternalOutput")
        self.hs = k.dram("hs", [8, 128, NTOK], F32, "Internal")
        k.psum_banks()
        self.cnt = {}
        self.alloc()

    def alloc(self):
        k = self.k
        sb = k.sbuf
        self.ones = sb("ones", [128, 128], BF16)
        self.BO = sb("BO", [128, 128], BF16)
        self.permf = sb("permf", [128, 2, 128], F32)
        self.perm = sb("perm", [128, 2, 128], BF16)
        self.cst = sb("cst", [128, 5, 128], F32)
        self.ccol = sb("ccol", [128, 2], F32)
        self.cT = sb("cT", [128, 8, 2], F32)
        self.cTb = sb("cTb", [128, 8, 2], BF16)
        self.bm = sb("bm", [128, 72, 2], F32)
        self.ng = sb("ng", [128, 6, 8, 2], F32)
        self.modT_l = [sb(f"modT{i}", [128, 3, 3, 8, 2], F32) for i in range(DEPTH)]
        self.mA_l = [sb(f"mA{i}", [128, 3, 8, 2], F32) for i in range(DEPTH)]
        self.mG_l = [sb(f"mG{i}", [128, 3, 8, 2], F32) for i in range(DEPTH)]
        self.epsc = sb("epsc", [128, 1], F32)
        self.one1 = sb("one1", [128, 1], F32)
        self.pv = sb("pv", [128, 8], F32)
        self.pwf = sb("pwf", [128, 2, 128], F32)
        self.pwb = sb("pwb", [128, 2, 128], BF16)
        self.rdl = sb("rdl", [128, 8], F32)
        self.lg = sb("lg", [128, 8], F32)
        self.lgpp = sb("lgpp", [128, 4], F32)
        self.g128 = sb("g128", [128, 4], F32)
        self.dl = sb("dl", [128, 128], F32)
        self.lamt = sb("lamt", [128, 4], F32)
        self.neglam = sb("neglam", [128, 1], F32)
        self.tabq = sb("tabq", [128, 2, 2, 128], F32)
        self.tabk = sb("tabk", [128, 2, 4], F32)
        self.mask = sb("mask", [128, 4, 128], F32)
        self.mtmp = sb("mtmp", [128, 128], F32)
        ARENA = 196096
        self.arena = sb("arena", [128, ARENA // 4], F32)
        self.atop = 0

        def view(nbytes, dt, pat=None, **kw):
            off = self.atop
            assert off % 4 == 0 and nbytes % 4 == 0
            self.atop += nbytes
            assert self.atop <= ARENA, (self.atop, ARENA)
            v = self.arena[:, off // 4:(off + nbytes) // 4]
            if dt is BF16:
                v = v.bitcast(BF16)
            if pat is not None:
                v = v.rearrange(pat, **kw)
            return v

        self.hm = view(8192, BF16, "p (a b) -> p a b", b=512)
        self.ff = view(16384, F32, "p (a b) -> p a b", b=512)
        self.t1 = [view(2048, F32) for _ in range(2)]
        self.sg = [view(2048, F32) for _ in range(2)]
        self.rstd = view(2048, F32)
        self.wst = [view(4096, F32) for _ in range(4)]
        self.wbf = [view(2048, BF16, "p (a b) -> p a b", b=128) for _ in range(4)]
        mark = self.atop
        self.hb = [view(16384, F32, "p (a b) -> p a b", b=512) for _ in range(2)]
        self.av = view(22528, BF16, "p (a b) -> p a b", b=512)
        self.wdst = [view(5632, F32) for _ in range(2)]
        self.wdbf = [view(5632, BF16, "p (a b) -> p a b", b=128) for _ in range(2)]
        self.mst = [view(4096, F32) for _ in range(2)]
        self.mbf = [view(2048, BF16, "p (a b) -> p a b", b=128) for _ in range(2)]
        self.sq2 = view(8192, BF16, "p (a b) -> p a b", b=512)
        self.rstd2 = view(2048, F32)
        ffn_top = self.atop
        self.atop = mark
        self.kA = view(4608, BF16)
        self.vAf = view(9216, BF16)
        self.vA = self.vAf.rearrange("p (t h e) -> p t h e", t=NT, h=2)
        self.kB = view(9216, BF16, "p (j t) -> p j t", j=2)
        self.vBf = view(18432, BF16)
        self.vB = self.vBf.rearrange("p (t h e) -> p t h e", t=NT, h=4)
        self.kD = view(9216, BF16, "p (j t) -> p j t", j=2)
        self.vD = view(9216, BF16, "p (t h e) -> p t h e", t=NT, h=4)
        self.KVB = view(9216, F32, "p (t j e) -> p t j e", t=NT, j=2)
        self.SfB = view(4608, BF16, "p (t j e) -> p t j e", t=NT, j=2)
        self.SbB = view(4608, BF16, "p (t j e) -> p t j e", t=NT, j=2)
        self.yC = view(9216, BF16, "p (j t) -> p j t", j=2)
        self.rope = view(8192, F32, "p (g c t) -> p g c t", g=2, c=2)
        mark2 = self.atop
        self.uC = view(9216, BF16, "p (j t) -> p j t", j=2)
        self.wtbf = view(14336, BF16, "p (a b) -> p a b", b=896)
        self.ptmp = [view(1216, F32) for _ in range(4)]
        self.ktmp = view(1024, BF16, "p (d h e) -> p d h e", d=2, h=4)
        self.S32f = view(1024, F32)
        self.S32 = self.S32f.rearrange("p (d j e) -> p d j e", d=2, j=2)
        self.invb = view(2048, F32, "p (j t) -> p j t", j=2)
        self.poolbs = [view(512, BF16) for _ in range(6)]
        self.kvsq = [view(1024, BF16) for _ in range(2)]
        kv_top = self.atop
        self.atop = mark2
        self.qA = view(2048, BF16, "p (j t) -> p j t", j=2)
        self.qB = view(2048, BF16, "p (j t) -> p j t", j=2)
        self.qD = view(2048, BF16, "p (j t) -> p j t", j=2)
        self.qfb = view(4096, BF16, "p (d j t) -> p d j t", d=2, j=2)
        self.gate = view(4096, F32, "p (j t) -> p j t", j=2)
        self.yblk = view(6144, BF16, "p (j t) -> p j t", j=6)
        self.PT = [view(1024, BF16) for _ in range(6)]
        self.ft = [view(2048, F32) for _ in range(4)]
        self.ydt = view(2048, F32)
        self.hc = [view(2048, F32) for _ in range(2)]
        self.arena_used = max(ffn_top, kv_top, self.atop)

    def rr(self, name, n):
        c = self.cnt.get(name, 0)
        self.cnt[name] = c + 1
        return c % n

    def prologue(self):
        k = self.k
        B = k.buf
        k.op("pool", I.memset(self.ones[:], 1.0), w=[B("ones")])
        k.op("pool", I.memset(self.epsc[:], EPS), w=[B("epsc")])
        k.op("pool", I.memset(self.one1[:], 1.0), w=[B("one1")])
        k.op("pool", I.memset(self.BO[:], 0.0), w=[B("BO")])
        k.op("pool", I.memset(self.BO[0:64, 0:64], 1.0), w=[B("BO")])
        k.op("pool", I.memset(self.BO[64:128, 64:128], 1.0), w=[B("BO")])
        k.dma("sp", self.cT[:], self.inp["cT"], w=[B("cT")])
        k.dma("sp", self.permf[:], self.inp["perm"], w=[B("permf")])
        k.dma("sp", self.cst[:], self.inp["cst"], w=[B("cst")])
        k.dma("sp", self.ccol[:], self.inp["ccol"], w=[B("ccol")])
        k.op("dve", I.tensor_copy(out=self.perm[:], in_=self.permf[:]), r=[B("permf")], w=[B("perm")])
        k.op("act", I.activation(out=self.cTb[:], in_=self.cT[:], func=AF.Silu), r=[B("cT")], w=[B("cTb")])

    def setlay(self, lay):
        self.modT, self.mA, self.mG = self.modT_l[lay], self.mA_l[lay], self.mG_l[lay]
        self.L = lay

    def mB(self, name):
        return self.k.buf(name, self.L)

    def mod(self, lay):
        for _ in self.mod_gen(lay, own=False):
            pass

    def mod_tick(self):
        g = getattr(self, "modgen", None)
        if g is None:
            return False
        try:
            next(g)
            return True
        except StopIteration:
            self.modgen = None
            return False

    def mod_gen(self, lay, own):
        k = self.k
        modT, mA, mG = self.modT_l[lay], self.mA_l[lay], self.mG_l[lay]
        B = lambda *a: (k.buf(a[0], lay) if a[0] in ("modT", "mA", "mG") else k.buf(*a))
        k.dma("sp", self.bm[:], self.inp[f"bm{lay}"], w=[B("bm")])
        k.dma("sp", self.ng[:], self.inp[f"ng{lay}"], w=[B("ng")])
        wm = self.inp[f"wm{lay}"]
        bank, bb = k.banks[7]
        if not own:
            self.plan_w([((("wm", lay), cb, 72), wm[cb]) for cb in range(72)])

        def issue(cb):
            s = cb % 2
            k.dma("sp", self.mst[s], wm[cb], w=[B("mst", s)])
            k.op("act", I.activation(out=self.mbf[s].rearrange("p a b -> p (a b)"), in_=self.mst[s], func=AF.Copy), r=[B("mst", s)], w=[B("mbf", s)])

        if own:
            issue(0)
        for cb in range(72):
            if own:
                wt, wb = self.mbf[cb % 2], B("mbf", cb % 2)
            else:
                s = self.load_w()
                wt, wb = self.wbf[s], B("wbf", s)
            for kc in range(8):
                k.op("pe", I.matmul(bank[:, 2 * cb:2 * cb + 2], wt[:, kc, :], self.cTb[:, kc, :], start=(kc == 0), stop=(kc == 7)),
                     r=[wb, B("cTb")], w=[bb])
            if own and cb + 1 < 72:
                issue(cb + 1)
            yield cb
        k.op("dve", I.tensor_tensor(out=modT[:].rearrange("p a b c d -> p (a b c d)"), in0=bank[:, 0:144],
                                              in1=self.bm[:].rearrange("p a b -> p (a b)"), op=ALU.add),
             r=[bb, B("bm")], w=[B("modT")])
        for sub in range(3):
            coef = 1.0 if sub == 1 else 0.5
            k.op("dve", I.scalar_tensor_tensor(out=mA[:, sub], in0=modT[:, sub, 1], scalar=1.0,
                                                                in1=self.ng[:, 2 * sub], op0=ALU.add, op1=ALU.mult),
                 r=[B("modT"), B("ng")], w=[B("mA")])
            k.op("dve", I.scalar_tensor_tensor(out=mG[:, sub], in0=modT[:, sub, 2], scalar=coef,
                                                                          in1=self.ng[:, 2 * sub + 1], op0=ALU.mult, op1=ALU.mult),
                 r=[B("modT"), B("ng")], w=[B("mG")])

    def rstd_from_ssq(self, bank, bb, T, nfeat, out_ap, out_buf):
        k = self.k
        B = k.buf
        k.op("act", I.activation(out=out_ap, in_=bank[:, :T], func=AF.Ln, scale=1.0 / nfeat, bias=self.epsc[:]),
             r=[bb, B("epsc")], w=[out_buf])
        k.op("act", I.activation(out=out_ap, in_=out_ap, func=AF.Exp, scale=-0.5), r=[out_buf], w=[out_buf])

    def rms_rstd(self, src, srcbuf, T):
        k = self.k
        B = k.buf
        bank, bb = k.banks[6]
        k.op("act", I.activation(out=self.hm[:, :, :T], in_=src, func=AF.Square), r=[srcbuf], w=[B("hm")])
        for kc in range(8):
            k.op("pe", I.matmul(bank[:, :T], self.ones[:], self.hm[:, kc, :T], start=(kc == 0), stop=(kc == 7)),
                 r=[B("hm"), B("ones")], w=[bb])
        self.rstd_from_ssq(bank, bb, T, D, self.rstd[:, :T], B("rstd"))

    def modulate(self, hb, hbuf, T, sub, kind):
        k = self.k
        B = k.buf
        mA, modT = self.mA, self.modT
        self.rms_rstd(hb[:, :, :T], hbuf, T)
        for kc in range(8):
            s = self.rr("t1", 2)
            k.op("dve", I.tensor_tensor(out=self.t1[s][:, :T], in0=hb[:, kc, :T], in1=self.rstd[:, :T], op=ALU.mult),
                 r=[hbuf, B("rstd")], w=[B("t1", s)])
            k.op("act", I.activation(out=self.hm[:, kc, :T], in_=self.t1[s][:, :T], func=AF.Identity,
                                                         scale=mA[:, sub, kc, kind:kind + 1], bias=modT[:, sub, 0, kc, kind:kind + 1]),
                 r=[B("t1", s), self.mB("mA"), self.mB("modT")], w=[B("hm")])

    def plan_w(self, items, depth=2):
        self.wplan = list(items)
        self.wpos = 0
        self.wiss = 0
        self.wslot = {}
        self.wdepth = depth

    def _cache(self, group, n, width):
        if not hasattr(self, "wcache"):
            self.wcache = {}
            self.wcached = set()
        if group not in self.wcache:
            name = "wc_" + "_".join(str(x) for x in group)
            self.wcache[group] = self.k.dram(name, [n, 128, width], BF16, "Internal")
        return self.wcache[group], self.k.buf("wcache", group)

    def _issue_w(self, i):
        k = self.k
        B = k.buf
        (group, idx, n), src = self.wplan[i]
        s = self.rr("wst", 4)
        once = group[0] == "wm"
        if not once:
            cache, cbuf = self._cache(group, n, 1024)
        flat = self.wbf[s].rearrange("p a b -> p (a b)")
        if (not once) and (group, idx) in self.wcached:
            k.dma("sp", flat, cache[idx], r=[cbuf], w=[B("wbf", s)])
        else:
            k.dma("sp", self.wst[s], src, w=[B("wst", s)])
            k.op("act", I.activation(out=flat, in_=self.wst[s], func=AF.Copy), r=[B("wst", s)], w=[B("wbf", s)])
            if not once:
                k.dma("act", cache[idx], flat, r=[B("wbf", s)], w=[cbuf], sembuf=B("wbf", s))
                self.wcached.add((group, idx))
        self.wslot[i] = s

    def load_w(self, src_ap=None):
        i = self.wpos
        self.wpos += 1
        while self.wiss <= min(i + self.wdepth, len(self.wplan) - 1):
            self._issue_w(self.wiss)
            self.wiss += 1
        return self.wslot.pop(i)

    def load_wd(self, wd, oc, group):
        k = self.k
        B = k.buf
        s = self.rr("wd", 2)
        cache, cbuf = self._cache(group, 8, 2816)
        flat = self.wdbf[s].rearrange("p a b -> p (a b)")
        if (group, oc) in self.wcached:
            k.dma("sp", flat, cache[oc], r=[cbuf], w=[B("wdbf", s)])
            return s
        for half in range(2):
            s1 = self.rr("wdst", 2)
            k.dma("sp", self.wdst[s1], wd[oc][:, half * 1408:(half + 1) * 1408], w=[B("wdst", s1)])
            k.op("act", I.activation(out=self.wdbf[s][:, half * 11:(half + 1) * 11, :].rearrange("p a b -> p (a b)"), in_=self.wdst[s1], func=AF.Copy),
                 r=[B("wdst", s1)], w=[B("wdbf", s)])
        k.dma("act", cache[oc], flat, r=[B("wdbf", s)], w=[cbuf], sembuf=B("wdbf", s))
        self.wcached.add((group, oc))
        return s

    def ffn_p2(self, T, lay, widx):
        k = self.k
        B = k.buf
        wd = self.inp[f"wd{lay}{widx}"]
        wds = None
        for fc in range(NFC):
            sl = [self.load_w(), self.load_w()]
            pb = self.rr("gu", 2)
            (bg, bgb), (bu, bub) = k.banks[2 * pb], k.banks[2 * pb + 1]
            for (bank, bbuf), s in ((k.banks[2 * pb], sl[0]), (k.banks[2 * pb + 1], sl[1])):
                for kc in range(8):
                    k.op("pe", I.matmul(bank[:, :T], self.wbf[s][:, kc, :], self.hm[:, kc, :T], start=(kc == 0), stop=(kc == 7)),
                         r=[B("wbf", s), B("hm")], w=[bbuf])
            s2 = self.rr("sg", 2)
            k.op("act", I.activation(out=self.sg[s2][:, :T], in_=bg[:, :T], func=AF.Silu), r=[bgb], w=[B("sg", s2)])
            k.op("dve", I.tensor_tensor(out=self.av[:, fc, :T], in0=bu[:, :T], in1=self.sg[s2][:, :T], op=ALU.mult),
                 r=[bub, B("sg", s2)], w=[B("av")])
            if fc == NFC - 3:
                wds = self.load_wd(wd, 0, ("wd", lay, widx))
            self.mod_tick()
        return wds

    def ffn_p3(self, T, lay, widx, wds):
        k = self.k
        B = k.buf
        wd = self.inp[f"wd{lay}{widx}"]
        b6, bb6 = k.banks[6]
        pend = None
        for oc in range(8):
            s = wds
            if oc + 1 < 8:
                wds = self.load_wd(wd, oc + 1, ("wd", lay, widx))
            bank, bbuf = k.banks[4 + self.rr("dn", 2)]
            for fc in range(NFC):
                k.op("pe", I.matmul(bank[:, :T], self.wdbf[s][:, fc, :], self.av[:, fc, :T], start=(fc == 0), stop=(fc == NFC - 1)),
                     r=[B("wdbf", s), B("av")], w=[bbuf])
            if pend is not None:
                pend()
            k.op("dve", I.tensor_copy(out=self.ff[:, oc, :T], in_=bank[:, :T]), r=[bbuf], w=[B("ff", oc)])
            k.op("act", I.activation(out=self.sq2[:, oc, :T], in_=self.ff[:, oc, :T], func=AF.Square), r=[B("ff", oc)], w=[B("sq2", oc)])

            def ssq(oc=oc):
                k.op("pe", I.matmul(b6[:, :T], self.ones[:], self.sq2[:, oc, :T], start=(oc == 0), stop=(oc == 7)),
                     r=[B("sq2", oc), B("ones")], w=[bb6])
            pend = ssq
        pend()

    def ffn_p4(self, hb, hbuf, T, kind, lay, sub):
        k = self.k
        B = k.buf
        self.setlay(lay)
        mG = self.mG
        b6, bb6 = k.banks[6]
        self.rstd_from_ssq(b6, bb6, T, D, self.rstd2[:, :T], B("rstd2"))
        for kc in range(8):
            s = self.rr("t1", 2)
            k.op("dve", I.scalar_tensor_tensor(out=self.t1[s][:, :T], in0=self.ff[:, kc, :T],
                                                                    scalar=mG[:, sub, kc, kind:kind + 1], in1=self.rstd2[:, :T],
                                                                    op0=ALU.mult, op1=ALU.mult),
                 r=[B("ff", kc), self.mB("mG"), B("rstd2")], w=[B("t1", s)])
            k.op("pool", I.tensor_tensor(out=hb[:, kc, :T], in0=hb[:, kc, :T], in1=self.t1[s][:, :T], op=ALU.add),
                 r=[B("t1", s), hbuf], w=[hbuf])

    def blocks(self, with_ctx=True):
        bl = []
        if with_ctx:
            bl.append((0, NCTX, CTX))
        for j in range(NLAT // 512):
            bl.append((NCTX + 512 * j, 512, LAT))
        return bl

    def stage_F(self, src, post_of=None, pre_of=None, with_ctx=True, dst=None):
        k = self.k
        B = k.buf
        blocks = self.blocks(with_ctx)
        if with_ctx:
            blocks = blocks[1:] + blocks[:1]
        ffns = ([(post_of, 2, 1)] if post_of is not None else []) + ([(pre_of, 0, 0)] if pre_of is not None else [])
        tasks = []
        if len(ffns) == 1:
            tasks = [(bi, 0) for bi in range(len(blocks))]
        else:
            for p in range(0, len(blocks), 2):
                grp = list(range(p, min(p + 2, len(blocks))))
                for fi in range(2):
                    tasks += [(bi, fi) for bi in grp]
        items = []
        for (bi, fi) in tasks:
            ly, sub, wi = ffns[fi]
            for fc in range(NFC):
                items += [((("wg", ly, wi), fc, NFC), self.inp[f"wg{ly}{wi}"][fc]), ((("wu", ly, wi), fc, NFC), self.inp[f"wu{ly}{wi}"][fc])]
        self.plan_w(items)

        def hbof(bi):
            return self.hb[bi % 2], B("hb", bi % 2)

        def load(bi):
            t0, T, kind = blocks[bi]
            hb, hbuf = hbof(bi)
            k.dma("sp", hb[:, :, :T], src[:, :, t0:t0 + T].rearrange("c p t -> p c t"), r=[B("hs", t0)] if src is self.hs else [], w=[hbuf])

        def store(bi):
            t0, T, kind = blocks[bi]
            hb, hbuf = hbof(bi)
            if dst is None:
                k.dma("sp", self.hs[:, :, t0:t0 + T].rearrange("c p t -> p c t"), hb[:, :, :T], r=[hbuf], w=[B("hs", t0)], sembuf=hbuf)
            else:
                k.dma("sp", dst[:, :, t0 - NCTX:t0 - NCTX + T].rearrange("c p t -> p c t"), hb[:, :, :T], r=[hbuf], w=[B("outd")], sembuf=hbuf)

        def p1(ti):
            bi, fi = tasks[ti]
            t0, T, kind = blocks[bi]
            ly, sub, wi = ffns[fi]
            if fi == 0:
                load(bi)
            hb, hbuf = hbof(bi)
            self.setlay(ly)
            self.modulate(hb, hbuf, T, sub, kind)

        p1(0)
        for ti, (bi, fi) in enumerate(tasks):
            t0, T, kind = blocks[bi]
            ly, sub, wi = ffns[fi]
            hb, hbuf = hbof(bi)
            wds = self.ffn_p2(T, ly, wi)
            hoist = ti + 1 < len(tasks) and tasks[ti + 1][0] != bi
            if hoist:
                p1(ti + 1)
            self.ffn_p3(T, ly, wi, wds)
            self.ffn_p4(hb, hbuf, T, kind, ly, sub)
            if fi == len(ffns) - 1:
                store(bi)
            if ti + 1 < len(tasks) and not hoist:
                p1(ti + 1)

    def prep_mixer(self, lay):
        k = self.k
        B = k.buf
        k.dma("sp", self.pv[:], self.inp[f"pv{lay}"], w=[B("pv")])
        k.dma("sp", self.pwf[:], self.inp[f"pw{lay}"], w=[B("pwf")])
        k.dma("sp", self.rdl[:], self.inp[f"rdl{lay}"], w=[B("rdl")])
        k.dma("sp", self.dl[:], self.inp[f"dl{lay}"], w=[B("dl")])
        k.op("dve", I.tensor_copy(out=self.pwb[:], in_=self.pwf[:]), r=[B("pwf")], w=[B("pwb")])
        k.op("act", I.activation(out=self.lg[:], in_=self.rdl[:], func=AF.Exp, scale=-1.0), r=[B("rdl")], w=[B("lg")])
        k.op("act", I.activation(out=self.lg[:], in_=self.lg[:], func=AF.Ln, bias=self.one1[:]), r=[B("lg"), B("one1")], w=[B("lg")])
        k.op("dve", I.tensor_scalar(out=self.lg[:], in0=self.lg[:], scalar1=-1.0, scalar2=None, op0=ALU.mult), r=[B("lg")], w=[B("lg")])
        for d in range(2):
            for j in range(2):
                for half in range(2):
                    pr = slice(half * 64, half * 64 + 64)
                    src = d * 4 + 2 * j + half
                    k.op("dve", I.tensor_copy(out=self.lgpp[pr, d * 2 + j:d * 2 + j + 1], in_=self.lg[pr, src:src + 1]),
                         r=[B("lg")], w=[B("lgpp")])
        k.op("act", I.activation(out=self.g128[:], in_=self.lgpp[:], func=AF.Exp, scale=128.0), r=[B("lgpp")], w=[B("g128")])
        for d in range(2):
            for j in range(2):
                k.op("act", I.activation(out=self.tabq[:, d, j, :], in_=self.cst[:, 3 + d, :], func=AF.Exp,
                                                           scale=self.lgpp[:, d * 2 + j:d * 2 + j + 1]),
                     r=[B("cst"), B("lgpp")], w=[B("tabq")])
        for d in range(2):
            k.op("act", I.activation(out=self.tabk[:, d, :], in_=self.lg[:, d * 4:d * 4 + 4], func=AF.Exp, scale=self.ccol[:, d:d + 1]),
                 r=[B("lg"), B("ccol")], w=[B("tabk")])
        k.op("dve", I.tensor_scalar(out=self.tabk[:], in0=self.tabk[:], scalar1=0.125, scalar2=None, op0=ALU.mult), r=[B("tabk")], w=[B("tabk")])
        for h in range(4):
            k.op("act", I.activation(out=self.mask[:, h, :], in_=self.cst[:, 0, :], func=AF.Exp, scale=self.lg[:, h:h + 1]),
                 r=[B("cst"), B("lg")], w=[B("mask")])
            k.op("act", I.activation(out=self.mtmp[:], in_=self.cst[:, 1, :], func=AF.Exp, scale=self.lg[:, 4 + h:5 + h]),
                 r=[B("cst"), B("lg")], w=[B("mtmp")])
            k.op("dve", I.tensor_tensor(out=self.mask[:, h, :], in0=self.mask[:, h, :], in1=self.mtmp[:], op=ALU.mult),
                 r=[B("mask"), B("mtmp")], w=[B("mask")])
            k.op("dve", I.tensor_tensor(out=self.mask[:, h, :], in0=self.mask[:, h, :], in1=self.cst[:, 2, :], op=ALU.add),
                 r=[B("mask"), B("cst")], w=[B("mask")])
        k.op("dve", I.tensor_scalar(out=self.mask[:], in0=self.mask[:], scalar1=0.125, scalar2=None, op0=ALU.mult), r=[B("mask")], w=[B("mask")])
        lam_init = 0.8 - 0.6 * math.exp(-0.3 * lay)
        for i in range(2):
            k.op("dve", I.tensor_tensor(out=self.mtmp[:, i * 32:(i + 1) * 32], in0=self.dl[:, 64 * i:64 * i + 32],
                                                       in1=self.dl[:, 64 * i + 32:64 * i + 64], op=ALU.mult),
                 r=[B("dl")], w=[B("mtmp")])
            k.op("dve", I.tensor_reduce(out=self.lamt[:, i:i + 1], in_=self.mtmp[:, i * 32:(i + 1) * 32], axis=mybir.AxisListType.X, op=ALU.add),
                 r=[B("mtmp")], w=[B("lamt")])
        k.op("act", I.activation(out=self.lamt[:, 2:4], in_=self.lamt[:, 0:2], func=AF.Exp), r=[B("lamt")], w=[B("lamt")])
        k.op("dve", I.scalar_tensor_tensor(out=self.neglam[:], in0=self.lamt[:, 3:4], scalar=-lam_init, in1=self.lamt[:, 2:3],
                                                     op0=ALU.add, op1=ALU.subtract),
             r=[B("lamt")], w=[B("neglam")])
        self.lam_init = lam_init

    def load_rope(self, t0, T):
        k = self.k
        B = k.buf
        l0 = t0 - NCTX
        k.dma("sp", self.rope[:, 0, :, :T], self.inp["ropeA"][:, :, l0:l0 + T].rearrange("c p t -> p c t"), w=[B("rope")])
        k.dma("sp", self.rope[:, 1, :, :T], self.inp["ropeB"][:, :, l0:l0 + T].rearrange("c p t -> p c t"), w=[B("rope")])

    def proj_fm(self, lay, name, T, bank_i):
        k = self.k
        B = k.buf
        s = self.load_w(self.inp[f"winf{lay}"][FM_IDX[name]])
        bank, bb = k.banks[bank_i]
        for kc in range(8):
            k.op("pe", I.matmul(bank[:, :T], self.wbf[s][:, kc, :], self.hm[:, kc, :T], start=(kc == 0), stop=(kc == 7)),
                 r=[B("wbf", s), B("hm")], w=[bb])
        return bank, bb

    def headnorm(self, src_ap, src_bufs, T, gain_ap, gain_bufs, out_ap, out_buf, imm=1.0, sq_eng_in_psum=True):
        k = self.k
        B = k.buf
        bank, bb = k.banks[6]
        sqb = self.PTs()
        k.op("act", I.activation(out=sqb[0][:, :T], in_=src_ap, func=AF.Square), r=src_bufs, w=[sqb[1]])
        k.op("pe", I.matmul(bank[:, :T], self.BO[:], sqb[0][:, :T], start=True, stop=True), r=[sqb[1], B("BO")], w=[bb])
        self.rstd_from_ssq(bank, bb, T, 64, self.rstd[:, :T], B("rstd"))
        if imm != 1.0:
            k.op("dve", I.tensor_scalar(out=self.rstd[:, :T], in0=self.rstd[:, :T], scalar1=float(imm), scalar2=None, op0=ALU.mult),
                 r=[B("rstd")], w=[B("rstd")])
        k.op("dve", I.scalar_tensor_tensor(out=out_ap, in0=src_ap, scalar=gain_ap, in1=self.rstd[:, :T], op0=ALU.mult, op1=ALU.mult),
             r=list(src_bufs) + list(gain_bufs) + [B("rstd")], w=[out_buf])

    WARM_N = 0

    def warm(self):
        k = self.k
        b7, bb7 = k.banks[7]
        for _ in range(self.WARM_N):
            k.op("pe", I.matmul(b7[:, 0:128], self.ones[:], self.ones[:], start=True, stop=True), r=[k.buf("ones")], w=[bb7])

    def PTs(self):
        if self.stage == "KV":
            s = self.rr("kvsq", 2)
            return (self.kvsq[s], self.k.buf("kvsq", s))
        s = self.rr("PT", 6)
        return (self.PT[s], self.k.buf("PT", s))

    def rope_apply(self, xb_ap, xb_buf, T, g, out_ap, out_buf):
        k = self.k
        B = k.buf
        bank, bb = k.banks[7]
        k.op("pe", I.matmul(bank[:, :T], self.perm[:, g, :], xb_ap, start=True, stop=True), r=[xb_buf, B("perm")], w=[bb])
        f0, f1 = self.fts(), self.fts()
        k.op("dve", I.tensor_tensor(out=f0[0][:, :T], in0=xb_ap, in1=self.rope[:, g, 0, :T], op=ALU.mult),
             r=[xb_buf, B("rope")], w=[f0[1]])
        k.op("dve", I.tensor_tensor(out=f1[0][:, :T], in0=bank[:, :T], in1=self.rope[:, g, 1, :T], op=ALU.mult),
             r=[bb, B("rope")], w=[f1[1]])
        k.op("pool", I.tensor_tensor(out=out_ap, in0=f0[0][:, :T], in1=f1[0][:, :T], op=ALU.add), r=[f0[1], f1[1]], w=[out_buf])

    def fts(self):
        if self.stage == "KV":
            s = self.rr("kvft", 4)
            return self.kvft[s]
        s = self.rr("ft", 4)
        return (self.ft[s], self.k.buf("ft", s))

    def stage_KV(self, lay):
        k = self.k
        B = k.buf
        self.stage = "KV"
        self.setlay(lay)
        self.kvft = [(self.t1[0], B("t1", 0)), (self.t1[1], B("t1", 1)), (self.sg[0], B("sg", 0)), (self.sg[1], B("sg", 1))]
        hbm, hbuf = self.ff, B("ff")
        for i in range(7):
            s = self.rr("wst", 4)
            k.dma("sp", self.wst[s], self.inp[f"wt{lay}"][i], w=[B("wst", s)])
            k.op("dve", I.tensor_copy(out=self.wtbf.rearrange("p a b -> p (a b)")[:, i * 1024:(i + 1) * 1024], in_=self.wst[s]),
                 r=[B("wst", s)], w=[B("wtbf")])
        k.op("pool", I.memset(self.vAf, 1.0), w=[B("vA")])
        k.op("pool", I.memset(self.vBf, 1.0), w=[B("vB")])
        k.op("pool", I.memset(self.S32f, 0.0), w=[B("S32")])
        for i_ in range(4):
            k.op("pool", I.memset(self.ptmp[i_], 0.0), w=[B("ptmp", i_)])
        self.pool_pending = []
        wf = self.inp[f"winf{lay}"]
        self.plan_w([((("winf", lay), FM_IDX[n], 15), wf[FM_IDX[n]]) for _ in self.blocks(True) for n in ("AK", "BK0", "BK1", "C0", "C1", "DK0", "DK1")])
        for (t0, T, kind) in self.blocks(True):
            k.dma("sp", hbm[:, :, :T], self.hs[:, :, t0:t0 + T].rearrange("c p t -> p c t"), r=[B("hs", t0)], w=[hbuf])
            self.modulate(hbm, hbuf, T, 1, kind)
            if kind == LAT:
                self.load_rope(t0, T)
                self.chk("KVl")
            bank, bb = self.proj_fm(lay, "AK", T, self.rr("pj", 2))
            if kind == CTX:
                self.headnorm(bank[:, :T], [bb], T, self.pv[:, 1:2], [B("pv")], self.kA[:, t0:t0 + T], B("kA"))
            else:
                xb = self.PTs()
                self.headnorm(bank[:, :T], [bb], T, self.pv[:, 1:2], [B("pv")], xb[0][:, :T], xb[1])
                self.rope_apply(xb[0][:, :T], xb[1], T, 0, self.kA[:, t0:t0 + T], B("kA"))
            for j in range(2):
                bank, bb = self.proj_fm(lay, f"BK{j}", T, self.rr("pj", 2))
                if kind == CTX:
                    k.op("act", I.activation(out=self.kB[:, j, t0:t0 + T], in_=bank[:, :T], func=AF.Copy), r=[bb], w=[B("kB")])
                else:
                    xb = self.PTs()
                    k.op("act", I.activation(out=xb[0][:, :T], in_=bank[:, :T], func=AF.Copy), r=[bb], w=[xb[1]])
                    self.rope_apply(xb[0][:, :T], xb[1], T, 1, self.kB[:, j, t0:t0 + T], B("kB"))
            for name, dst, dbuf in (("C0", self.uC[:, 0, t0:t0 + T], B("uC")), ("C1", self.uC[:, 1, t0:t0 + T], B("uC")),
                                    ("DK0", self.kD[:, 0, t0:t0 + T], B("kD")), ("DK1", self.kD[:, 1, t0:t0 + T], B("kD"))):
                bank, bb = self.proj_fm(lay, name, T, self.rr("pj", 2))
                k.op("act", I.activation(out=dst, in_=bank[:, :T], func=AF.Copy), r=[bb], w=[dbuf])
            self.pool_finish()
            if kind == CTX:
                self.pool_blocks([(0, 256, CTX)])
            else:
                l = (t0 - NCTX) // 512
                subs = ([2 * l - 1] if l >= 1 else []) + [2 * l] + ([7] if l == 3 else [])
                self.pool_blocks([(NCTX + 256 * s_, 256, LAT) for s_ in subs])
            for tl in range(T // 128):
                gt = t0 // 128 + tl
                cs = slice(tl * 128, (tl + 1) * 128)
                b2, bb2 = k.banks[2]
                b3, bb3 = k.banks[3]
                for kc in range(8):
                    k.op("pe", I.matmul(b2[:, 0:384], self.hm[:, kc, cs], self.wtbf[:, kc, 0:384], start=(kc == 0), stop=(kc == 7)),
                         r=[B("hm"), B("wtbf")], w=[bb2])
                for kc in range(8):
                    k.op("pe", I.matmul(b3[:, 0:512], self.hm[:, kc, cs], self.wtbf[:, kc, 384:896], start=(kc == 0), stop=(kc == 7)),
                         r=[B("hm"), B("wtbf")], w=[bb3])
                k.op("act", I.activation(out=self.vA[:, gt, 0, 0:64], in_=b2[:, 0:64], func=AF.Copy), r=[bb2], w=[B("vA")])
                k.op("act", I.activation(out=self.vA[:, gt, 1, 64:128], in_=b2[:, 64:128], func=AF.Copy), r=[bb2], w=[B("vA")])
                for h in range(4):
                    o0 = 0 if h % 2 == 0 else 64
                    k.op("dve" if h < 2 else "act",
                         (I.tensor_copy(out=self.vB[:, gt, h, o0:o0 + 64], in_=b2[:, 128 + 64 * h:192 + 64 * h])) if h < 2 else
                         (I.activation(out=self.vB[:, gt, h, o0:o0 + 64], in_=b2[:, 128 + 64 * h:192 + 64 * h], func=AF.Copy)),
                         r=[bb2], w=[B("vB")])
                k.op("dve", I.tensor_copy(out=self.vD[:, gt].rearrange("p h e -> p (h e)"), in_=b3[:, 256:512]), r=[bb3], w=[B("vD")])
                self.chk("KVc1")
                for d in range(2):
                    for h in range(4):
                        k.op("dve", I.tensor_scalar(out=self.ktmp[:, d, h, :], in0=b3[:, 64 * h:64 * h + 64], scalar1=self.tabk[:, d, h:h + 1], scalar2=None,
                                                    op0=ALU.mult),
                             r=[bb3, B("tabk")], w=[B("ktmp")])
                self.chk("KVc2")
                b7, bb7 = k.banks[7]
                for d in range(2):
                    for h in range(4):
                        j, par = h // 2, h % 2
                        pr = slice(par * 64, par * 64 + 64)
                        k.op("pe", I.matmul(b7[pr, d * 128 + j * 64:d * 128 + j * 64 + 64], self.ktmp[:, d, h, :],
                                                                                 self.vD[:, gt, h, :], start=True, stop=True),
                             r=[B("ktmp"), B("vD")], w=[bb7])
                self.chk("KVc3")
                k.op("dve", I.tensor_copy(out=self.SfB[:, gt], in_=self.S32[:, 0]), r=[B("S32")], w=[B("SfB")])
                for j in range(2):
                    k.op("dve", I.scalar_tensor_tensor(out=self.S32[:, 0, j, :], in0=self.S32[:, 0, j, :], scalar=self.g128[:, j:j + 1],
                                                                     in1=b7[:, j * 64:j * 64 + 64], op0=ALU.mult, op1=ALU.add),
                         r=[B("S32"), B("g128"), bb7], w=[B("S32")])
                k.op("dve", I.tensor_copy(out=self.KVB[:, gt].rearrange("p j e -> p (j e)"), in_=b7[:, 128:256]), r=[bb7], w=[B("KVB")])
            self.chk("KVc")
        self.chk("KVd")
        self.pool_finish()
        order = [1, 0] + list(range(NT - 1, 1, -1))
        for i, t in enumerate(order):
            k.op("pool", I.tensor_copy(out=self.SbB[:, t], in_=self.S32[:, 1]), r=[B("S32")], w=[B("SbB")])
            if i + 1 < len(order):
                for j in range(2):
                    k.op("dve", I.scalar_tensor_tensor(out=self.S32[:, 1, j, :], in0=self.S32[:, 1, j, :], scalar=self.g128[:, 2 + j:3 + j],
                                                                          in1=self.KVB[:, t, j, :], op0=ALU.mult, op1=ALU.add),
                         r=[B("S32"), B("g128"), B("KVB")], w=[B("S32")])

    def pool_blocks(self, pblocks):
        k = self.k
        B = k.buf
        for (t0g, T, kind) in pblocks:
            L = NCTX if kind == CTX else NLAT
            seq0 = 0 if kind == CTX else NCTX
            t0 = t0g - seq0
            invsrc = self.inp["invc_ctx" if kind == CTX else "invc_lat"]
            k.dma("sp", self.invb[:, :, :T], invsrc[:, :, t0:t0 + T], w=[B("invb")])
            W = T + 48
            ta, tb = max(0, t0 - 16), min(L, t0 + T + 16)
            qa, qb = ta - t0 + 32, tb - t0 + 32
            for c in range(2):
                pi = self.rr("poolb", 6)
                P_, A_, B_, C_ = [p_[:, 0:W] for p_ in self.ptmp]
                pb = [B("ptmp", i) for i in range(4)]
                k.op("pool", I.memset(P_, 0.0), w=[pb[0]])
                k.op("pool", I.tensor_copy(out=P_[:, qa:qb], in_=self.uC[:, c, seq0 + ta:seq0 + tb]), r=[B("uC")], w=[pb[0]])
                k.op("pool", I.tensor_tensor(out=A_[:, 16:W], in0=P_[:, 16:W], in1=P_[:, 15:W - 1], op=ALU.add), r=[pb[0]], w=[pb[1]])
                k.op("pool", I.tensor_tensor(out=B_[:, 16:W], in0=A_[:, 16:W], in1=A_[:, 14:W - 2], op=ALU.add), r=[pb[1]], w=[pb[2]])
                if c == 0:
                    lo, lo_b, lo_sh = A_, pb[1], 0
                    up, up_b, up_sh = B_, pb[2], 1
                else:
                    k.op("pool", I.tensor_tensor(out=C_[:, 16:W], in0=B_[:, 16:W], in1=B_[:, 12:W - 4], op=ALU.add), r=[pb[2]], w=[pb[3]])
                    k.op("pool", I.tensor_tensor(out=A_[:, 16:W], in0=C_[:, 16:W], in1=C_[:, 8:W - 8], op=ALU.add), r=[pb[3]], w=[pb[1]])
                    lo, lo_b, lo_sh = C_, pb[3], 3
                    up, up_b, up_sh = A_, pb[1], 7
                for (pr, src, sbuf_, sh) in ((slice(0, 64), lo, lo_b, lo_sh), (slice(64, 128), up, up_b, up_sh)):
                    k.op("pool", I.tensor_tensor(out=src[pr, 32 + sh:32 + sh + T], in0=src[pr, 32 + sh:32 + sh + T],
                                                                                      in1=self.invb[pr, c, :T], op=ALU.mult),
                         r=[sbuf_, B("invb")], w=[sbuf_])
                    k.op("pool", I.tensor_tensor(out=self.poolbs[pi][pr, :T], in0=src[pr, 32 + sh:32 + sh + T],
                                                                                        in1=P_[pr, 32:32 + T], op=ALU.subtract),
                         r=[sbuf_, pb[0]], w=[B("poolb", pi)])
                self.pool_pending.append((pi, c, t0g, T))

    def pool_finish(self):
        k = self.k
        B = k.buf
        b7, bb7 = k.banks[7]
        for (pi, c, t0g, T) in self.pool_pending:
            k.op("pe", I.matmul(b7[:, :T], self.pwb[:, c, :], self.poolbs[pi][:, :T], start=True, stop=True), r=[B("pwb"), B("poolb", pi)], w=[bb7])
            k.op("act", I.activation(out=self.yC[:, c, t0g:t0g + T], in_=b7[:, :T], func=AF.Identity, scale=self.pv[:, 4 + c:5 + c]),
                 r=[bb7, B("pv")], w=[B("yC")])
        self.pool_pending = []


    def attn_tiles(self, kind):
        return list(range(2)) if kind == CTX else list(range(NT))

    def stage_Y(self, lay, with_ctx):
        k = self.k
        B = k.buf
        self.stage = "Y"
        self.setlay(lay)
        mG = self.mG
        hbm, hbuf = self.ff, B("ff")
        wo = self.inp[f"wo{lay}"]
        wf = self.inp[f"winf{lay}"]
        self.plan_w([x for _ in self.blocks(with_ctx) for x in
                     [((("winf", lay), FM_IDX[n], 15), wf[FM_IDX[n]]) for n in ("AQ0", "AQ1", "BQ0", "BQ1", "DQ0", "DQ1", "G0", "G1")]
                     + [((("wo", lay), oc, 8), wo[oc]) for oc in range(8)]])
        for (t0, T, kind) in self.blocks(with_ctx):
            tiles = self.attn_tiles(kind)
            nch = T // 128
            k.dma("sp", hbm[:, :, :T], self.hs[:, :, t0:t0 + T].rearrange("c p t -> p c t"), r=[B("hs", t0)], w=[hbuf])
            self.modulate(hbm, hbuf, T, 1, kind)
            if kind == LAT:
                self.load_rope(t0, T)
            for j in range(2):
                bank, bb = self.proj_fm(lay, f"AQ{j}", T, self.rr("pj", 4))
                if kind == CTX:
                    self.headnorm(bank[:, :T], [bb], T, self.pv[:, 0:1], [B("pv")], self.qA[:, j, :T], B("qA"))
                else:
                    xb = self.PTs()
                    self.headnorm(bank[:, :T], [bb], T, self.pv[:, 0:1], [B("pv")], xb[0][:, :T], xb[1])
                    self.rope_apply(xb[0][:, :T], xb[1], T, 0, self.qA[:, j, :T], B("qA"))
            for j in range(2):
                bank, bb = self.proj_fm(lay, f"BQ{j}", T, self.rr("pj", 4))
                if kind == CTX:
                    k.op("act", I.activation(out=self.qB[:, j, :T], in_=bank[:, :T], func=AF.Copy), r=[bb], w=[B("qB")])
                else:
                    xb = self.PTs()
                    k.op("act", I.activation(out=xb[0][:, :T], in_=bank[:, :T], func=AF.Copy), r=[bb], w=[xb[1]])
                    self.rope_apply(xb[0][:, :T], xb[1], T, 1, self.qB[:, j, :T], B("qB"))
            for j in range(2):
                bank, bb = self.proj_fm(lay, f"DQ{j}", T, self.rr("pj", 4))
                k.op("act", I.activation(out=self.qD[:, j, :T], in_=bank[:, :T], func=AF.Copy), r=[bb], w=[B("qD")])
                for d in range(2):
                    for ci in range(nch):
                        k.op("dve", I.tensor_tensor(out=self.qfb[:, d, j, ci * 128:(ci + 1) * 128], in0=bank[:, ci * 128:(ci + 1) * 128],
                                                    in1=self.tabq[:, d, j, :], op=ALU.mult),
                             r=[bb, B("tabq")], w=[B("qfb")])
            for j in range(2):
                bank, bb = self.proj_fm(lay, f"G{j}", T, self.rr("pj", 4))
                k.op("act", I.activation(out=self.gate[:, j, :T], in_=bank[:, :T], func=AF.Silu), r=[bb], w=[B("gate")])
            for c in range(2):
                obs = [k.banks[4], k.banks[5]]
                pend = []
                for i, st in enumerate(tiles):
                    newp = []
                    for par in range(2):
                        pr = slice(par * 64, par * 64 + 64)
                        sbank, sbb = k.banks[SCB[self.rr("sc6", 6)]]
                        k.op("pe", I.matmul(sbank[:, :T], self.kA[pr, st * 128:(st + 1) * 128], self.qA[pr, c, :T], start=True, stop=True),
                             r=[B("kA"), B("qA")], w=[sbb])
                        newp.append((sbank, sbb, par))
                    if len(pend) >= 4:
                        pend.pop(0)()
                        pend.pop(0)()
                    for (sbank, sbb, par) in newp:
                        pt = self.PTs()
                        k.op("act", I.activation(out=pt[0][:, :T], in_=sbank[:, :T], func=AF.Exp, scale=0.125), r=[sbb], w=[pt[1]])

                        def pv_mm(st=st, pt=pt, i=i, par=par):
                            ob, obb = obs[par]
                            k.op("pe", I.matmul(ob[:, :T], self.vA[:, st, par, :], pt[0][:, :T], start=(i == 0), stop=(i == len(tiles) - 1)),
                                 r=[B("vA"), pt[1]], w=[obb])
                            self.warm()
                        pend.append(pv_mm)
                for f in pend:
                    f()
                for par in range(2):
                    pr = slice(par * 64, par * 64 + 64)
                    dn = slice(64, 128) if par == 0 else slice(0, 64)
                    ob, obb = obs[par]
                    nt, dt_ = self.fts(), self.fts()
                    k.op("act", I.activation(out=nt[0][pr, :T], in_=ob[pr, :T], func=AF.Copy), r=[obb], w=[nt[1]])
                    k.op("act", I.activation(out=dt_[0][pr, :T], in_=ob[dn, :T], func=AF.Copy), r=[obb], w=[dt_[1]])
                    k.op("dve", I.reciprocal(out=dt_[0][pr, :T], in_=dt_[0][pr, :T]), r=[dt_[1]], w=[dt_[1]])
                    k.op("dve", I.tensor_tensor(out=self.yblk[pr, c, :T], in0=nt[0][pr, :T], in1=dt_[0][pr, :T], op=ALU.mult),
                         r=[nt[1], dt_[1]], w=[B("yblk")])
            scB = 32 ** -0.5
            for j in range(2):
                yd = (self.ydt, B("ydt"))
                for par in range(2):
                    h = 2 * j + par
                    pr = slice(par * 64, par * 64 + 64)
                    dn = slice(64, 128) if par == 0 else slice(0, 64)
                    obs = [k.banks[4], k.banks[5]]
                    pend = []
                    for i, st in enumerate(tiles):
                        newp = []
                        for mp in range(2):
                            base = par * 64 + mp * 32
                            sbank, sbb = k.banks[SCB[self.rr("sc6", 6)]]
                            tp = (96, 0) if base == 96 else None
                            k.op("pe", I.matmul(
                                sbank[:, :T], self.kB[base:base + 32, j, st * 128:(st + 1) * 128], self.qB[base:base + 32, j, :T],
                                start=True, stop=True, tile_position=tp),
                                r=[B("kB"), B("qB")], w=[sbb])
                            newp.append((sbank, sbb, mp))
                        if len(pend) >= 4:
                            pend.pop(0)()
                            pend.pop(0)()
                        for (sbank, sbb, mp) in newp:
                            pt = self.PTs()
                            k.op("act", I.activation(out=pt[0][:, :T], in_=sbank[:, :T], func=AF.Exp, scale=scB), r=[sbb], w=[pt[1]])

                            def pv_mm(st=st, pt=pt, i=i, h=h, mp=mp):
                                ob, obb = obs[mp]
                                k.op("pe", I.matmul(ob[:, :T], self.vB[:, st, h, :], pt[0][:, :T], start=(i == 0), stop=(i == len(tiles) - 1)),
                                     r=[B("vB"), pt[1]], w=[obb])
                                self.warm()
                            pend.append(pv_mm)
                    for f in pend:
                        f()
                    n1, d1, n2, d2 = self.fts(), self.fts(), self.fts(), self.fts()
                    (o1, o1b), (o2, o2b) = obs
                    k.op("act", I.activation(out=n1[0][pr, :T], in_=o1[pr, :T], func=AF.Copy), r=[o1b], w=[n1[1]])
                    k.op("act", I.activation(out=d1[0][pr, :T], in_=o1[dn, :T], func=AF.Copy), r=[o1b], w=[d1[1]])
                    k.op("act", I.activation(out=n2[0][pr, :T], in_=o2[pr, :T], func=AF.Copy), r=[o2b], w=[n2[1]])
                    k.op("act", I.activation(out=d2[0][pr, :T], in_=o2[dn, :T], func=AF.Copy), r=[o2b], w=[d2[1]])
                    k.op("dve", I.reciprocal(out=d1[0][pr, :T], in_=d1[0][pr, :T]), r=[d1[1]], w=[d1[1]])
                    k.op("dve", I.reciprocal(out=d2[0][pr, :T], in_=d2[0][pr, :T]), r=[d2[1]], w=[d2[1]])
                    k.op("dve", I.tensor_tensor(out=n1[0][pr, :T], in0=n1[0][pr, :T], in1=d1[0][pr, :T], op=ALU.mult), r=[n1[1], d1[1]], w=[n1[1]])
                    k.op("dve", I.tensor_tensor(out=n2[0][pr, :T], in0=n2[0][pr, :T], in1=d2[0][pr, :T], op=ALU.mult), r=[n2[1], d2[1]], w=[n2[1]])
                    k.op("dve", I.scalar_tensor_tensor(out=yd[0][pr, :T], in0=n2[0][pr, :T], scalar=self.neglam[pr, 0:1],
                                                       in1=n1[0][pr, :T], op0=ALU.mult, op1=ALU.add),
                         r=[n2[1], n1[1], B("neglam")], w=[yd[1]])
                self.headnorm(yd[0][:, :T], [yd[1]], T, self.pv[:, 2:3], [B("pv")], self.yblk[:, 2 + j, :T], B("yblk"), imm=1.0 - self.lam_init)
            for j in range(2):
                oL, oLb = k.banks[4]
                oU, oUb = k.banks[5]
                for cidx in range(nch):
                    gt = t0 // 128 + cidx
                    cs = slice(cidx * 128, (cidx + 1) * 128)
                    for par in range(2):
                        h = 2 * j + par
                        pr = slice(par * 64, par * 64 + 64)
                        ob, obb = (oL, oLb) if par == 0 else (oU, oUb)
                        sb_i = self.rr("sc", 4)
                        sbank, sbb = k.banks[sb_i]
                        k.op("pe", I.matmul(sbank[:, 0:128], self.kD[pr, j, gt * 128:(gt + 1) * 128],
                                                                                           self.qD[pr, j, cs], start=True, stop=True),
                             r=[B("kD"), B("qD")], w=[sbb])
                        pt = self.PTs()
                        k.op("dve", I.tensor_tensor(out=pt[0][:, 0:128], in0=sbank[:, 0:128], in1=self.mask[:, h, :], op=ALU.mult),
                             r=[sbb, B("mask")], w=[pt[1]])
                        k.op("pe", I.matmul(ob[pr, cs], self.vD[:, gt, h, :], pt[0][:, 0:128], start=True, stop=False),
                             r=[B("vD"), pt[1]], w=[obb])
                        k.op("pe", I.matmul(ob[pr, cs], self.SfB[pr, gt, j, :], self.qfb[pr, 0, j, cs], start=False, stop=False),
                             r=[B("SfB"), B("qfb")], w=[obb])
                        k.op("pe", I.matmul(ob[pr, cs], self.SbB[pr, gt, j, :], self.qfb[pr, 1, j, cs], start=False, stop=True),
                             r=[B("SbB"), B("qfb")], w=[obb])
                od = self.fts()
                k.op("act", I.activation(out=od[0][0:64, :T], in_=oL[0:64, :T], func=AF.Copy), r=[oLb], w=[od[1]])
                k.op("act", I.activation(out=od[0][64:128, :T], in_=oU[64:128, :T], func=AF.Copy), r=[oUb], w=[od[1]])
                on = self.fts()
                self.headnorm(od[0][:, :T], [od[1]], T, self.pv[:, 3:4], [B("pv")], on[0][:, :T], on[1])
                k.op("dve", I.tensor_tensor(out=self.yblk[:, 4 + j, :T], in0=on[0][:, :T], in1=self.gate[:, j, :T], op=ALU.mult),
                     r=[on[1], B("gate")], w=[B("yblk")])
            if getattr(self, "debug_y", False):
                if "dbg_y" not in self.__dict__:
                    self.dbg_y = k.dram("dbg_y", [6, 128, NTOK], BF16, "ExternalOutput")
                k.dma("sp", self.dbg_y[:, :, t0:t0 + T].rearrange("c p t -> p c t"), self.yblk[:, :, :T], r=[B("yblk")], w=[B("dbg_y")])
            ysrc = [(self.yblk[:, 0, :T], B("yblk")), (self.yblk[:, 1, :T], B("yblk")), (self.yblk[:, 2, :T], B("yblk")), (self.yblk[:, 3, :T], B("yblk")),
                    (self.yC[:, 0, t0:t0 + T], B("yC")), (self.yC[:, 1, t0:t0 + T], B("yC")), (self.yblk[:, 4, :T], B("yblk")), (self.yblk[:, 5, :T], B("yblk"))]
            for oc in range(8):
                s = self.load_w(wo[oc])
                bank, bb = k.banks[self.rr("sc", 4)]
                for kc in range(8):
                    yap, ybuf = ysrc[kc]
                    k.op("pe", I.matmul(bank[:, :T], self.wbf[s][:, kc, :], yap, start=(kc == 0), stop=(kc == 7)),
                         r=[B("wbf", s), ybuf], w=[bb])
                k.op("act", I.activation(out=self.ff[:, oc, :T], in_=bank[:, :T], func=AF.Copy), r=[bb], w=[B("ff")])
            self.rms_rstd(self.ff[:, :, :T], B("ff"), T)
            for kc in range(8):
                s = self.rr("hc", 2)
                hcb = B("hc", s)
                k.dma("sp", self.hc[s][:, :T], self.hs[kc, :, t0:t0 + T], r=[B("hs", t0)], w=[hcb])
                f = self.fts()
                k.op("dve", I.scalar_tensor_tensor(out=f[0][:, :T], in0=self.ff[:, kc, :T], scalar=mG[:, 1, kc, kind:kind + 1],
                                                                        in1=self.rstd[:, :T], op0=ALU.mult, op1=ALU.mult),
                     r=[B("ff"), self.mB("mG"), B("rstd")], w=[f[1]])
                k.op("pool", I.tensor_tensor(out=self.hc[s][:, :T], in0=self.hc[s][:, :T], in1=f[0][:, :T], op=ALU.add),
                     r=[f[1], hcb], w=[hcb])
                k.dma("sp", self.hs[kc, :, t0:t0 + T], self.hc[s][:, :T], r=[hcb], w=[B("hs2", t0)], sembuf=hcb)
        for (t0, T, kind) in self.blocks(with_ctx):
            B("hs", t0).writers.update(B("hs2", t0).writers)

    def dump(self, name, ap, shape, bufs, dt=F32):
        k = self.k
        d = k.dram("dbg_" + name, shape, dt, "ExternalOutput")
        k.dma("sp", d, ap, r=bufs, w=[k.buf("dbg_" + name)])

    def dump_hs(self):
        k = self.k
        d = k.dram("dbg_hs", [8, 128, NTOK], F32, "ExternalOutput")
        k.dma("sp", d, self.hs, r=[k.buf("hs", t0) for (t0, T, kind) in self.blocks(True)], w=[k.buf("dbg_hs")])

    def dump_bf(self, name, ap, shape, bufs):
        k = self.k
        raise NotImplementedError

    def build(self, stop_after=None):
        k = self.k
        self.stage = "F"
        self.prologue()
        self.mod(0)
        if stop_after == "MOD":
            self.dump("modT", self.modT[:].rearrange("p a b c d -> p (a b c d)"), [128, 144], [k.buf("modT", 0)])
            return k.emit()
        self.modgen = self.mod_gen(1, own=True)
        self.stage_F(self.inp["xT"], pre_of=0)
        while self.mod_tick():
            pass
        if stop_after == "F0":
            self.dump_hs()
            return k.emit()
        for lay in range(DEPTH):
            last = lay == DEPTH - 1
            k.barrier()
            self.prep_mixer(lay)
            if stop_after == f"PM{lay}":
                self.dump("mask", self.mask[:].rearrange("p a b -> p (a b)"), [128, 512], [k.buf("mask")])
                self.dump("tabq", self.tabq[:].rearrange("p a b c -> p (a b c)"), [128, 512], [k.buf("tabq")])
                self.dump("tabk", self.tabk[:].rearrange("p a b -> p (a b)"), [128, 8], [k.buf("tabk")])
                self.dump("neglam", self.neglam[:], [128, 1], [k.buf("neglam")])
                self.dump("g128", self.g128[:], [128, 4], [k.buf("g128")])
                return k.emit()
            self.kv_stop = stop_after
            try:
                self.stage_KV(lay)
            except _Stop:
                return k.emit()
            if stop_after == f"KV{lay}":
                return k.emit()
            k.barrier()
            self.stage_Y(lay, with_ctx=not last)
            if stop_after == f"Y{lay}":
                self.dump_hs()
                if getattr(self, "debug_y", False):
                    self.dump("yC", self.yC, [128, 2, NTOK], [k.buf("yC")], dt=BF16)
                return k.emit()
            k.barrier()
            self.stage = "F"
            if not last:
                self.stage_F(self.hs, post_of=lay, pre_of=lay + 1, with_ctx=True)
            else:
                self.stage_F(self.hs, post_of=lay, with_ctx=False, dst=self.out)
            if stop_after == f"F{lay + 1}":
                self.dump_hs()
                return k.emit()
        return k.emit()


def _shapes(sh, per0):
    shapes = {}
    for name, a in list(sh.items()) + list(per0.items()):
        shapes[name] = (a.shape, F32)
    return shapes


def kernel(**inputs):
    inputs = {k_: np.asarray(v) for k_, v in inputs.items()}
    sh, per = host_prep(inputs)
    prog = Prog(_shapes(sh, per[0]))
    nc = prog.build()
    in_maps = [{k_: v for k_, v in dict(sh, **p).items() if k_ in prog.inp} for p in per]
    res = run_bass_kernel_spmd(nc, in_maps, core_ids=list(range(8)))
    outs = []
    for r in res.results:
        o = np.asarray(r["out"]).reshape(D, NLAT)
        outs.append(o.T)
    return np.ascontiguousarray(np.stack(outs, axis=0)).astype(np.float32)
```

```python
import math
import numpy as np
from contextlib import ExitStack
import concourse.bass as bass
import concourse.mybir as mybir
from concourse.bass_utils import run_bass_kernel_spmd

F32 = mybir.dt.float32
BF16 = mybir.dt.bfloat16
AF = mybir.ActivationFunctionType
ALU = mybir.AluOpType

class _Rec:
    def __getattr__(self, name):
        def mk(*a, **kw):
            return lambda e: getattr(e, name)(*a, **kw)
        return mk


I = _Rec()

D = 1024
NLAT = 2048
NCTX = 256
NTOK = NLAT + NCTX
DFF = 2816
NFC = DFF // 128
DEPTH = 2
EPS = 1e-6


class Buf:
    __slots__ = ("name", "writers", "readers", "sem", "ntick", "last_dma")

    def __init__(self, name):
        self.name = name
        self.writers = {}
        self.readers = {}
        self.sem = None
        self.ntick = 0
        self.last_dma = None


class Ins:
    __slots__ = ("eng", "fn", "deps", "idx", "sig", "tick", "is_dma", "sembuf", "dtick", "where")


class KB:
    ENG = ("pe", "act", "dve", "pool", "sp")

    def __init__(self):
        self.nc = bass.Bass("TRN2", target_bir_lowering=False)
        self.es = ExitStack()
        self.q = {e: [] for e in self.ENG}
        self.bufs = {}
        self.banks = []
        self.nsb = 0
        self.fence = {}
        self.where = {}

    def buf(self, *key):
        b = self.bufs.get(key)
        if b is None:
            b = self.bufs[key] = Buf(key)
        return b

    def sbuf(self, name, shape, dt):
        return self.es.enter_context(self.nc.sbuf_tensor("sb_" + name, list(shape), dt))

    def psum_banks(self):
        for i in range(8):
            t = self.es.enter_context(self.nc.psum_tensor(f"bank{i}", [128, 512], F32))
            self.banks.append((t, self.buf("bank", i)))

    def dram(self, name, shape, dt, kind):
        return self.nc.dram_tensor(name, list(shape), dt, kind=kind).ap()

    def _add(self, eng, fn, r, w, is_dma=False, sembuf=None):
        ins = Ins()
        ins.eng = eng
        ins.fn = fn
        ins.is_dma = is_dma
        ins.sig = False
        ins.tick = 0
        ins.sembuf = sembuf
        ins.dtick = 0
        ins.idx = len(self.q[eng])
        import sys as _sys
        f_ = _sys._getframe(2)
        ins.where = f"{f_.f_code.co_name}:{f_.f_lineno}"
        deps = set()
        for b in r:
            deps.update(b.writers.values())
            if b.name[0] == "bank":
                deps.update(v for kk, v in b.readers.items() if kk != eng)
        for b in w:
            deps.update(b.writers.values())
            deps.update(b.readers.values())
        if self.fence.get(eng):
            deps.update(self.fence.pop(eng))
        if is_dma:
            if sembuf.last_dma is not None:
                deps.add(sembuf.last_dma)
            sembuf.last_dma = ins
            sembuf.ntick += 1
            ins.dtick = 16 * sembuf.ntick
        ins.deps = deps
        key = ("dma", id(sembuf)) if is_dma else eng
        for b in w:
            if b.readers:
                b.writers = {key: ins}
                b.readers = {}
            else:
                b.writers[key] = ins
        for b in r:
            b.readers[key] = ins
        self.q[eng].append(ins)
        return ins

    def barrier(self):
        fence = set()
        for e in self.ENG:
            if self.q[e]:
                fence.add(self.q[e][-1])
        for b in self.bufs.values():
            if b.last_dma is not None:
                fence.add(b.last_dma)
        self.fence = {e: set(fence) for e in self.ENG}

    def op(self, eng, fn, r=(), w=()):
        return self._add(eng, fn, list(r), list(w))

    def dma(self, queue, out, in_, r=(), w=(), sembuf=None):
        r = list(r)
        w = list(w)
        if sembuf is None:
            sembuf = w[0] if w else r[0]
        return self._add(queue, I.dma_start(out=out, in_=in_), r, w, True, sembuf)

    def _needs_wait(self, ins, d):
        if d.is_dma:
            return True
        if d.eng != ins.eng:
            return True
        if ins.is_dma:
            return True
        if ins.eng == "pe":
            return False
        return ins.idx - d.idx <= 2

    def emit(self):
        nc = self.nc
        es = self.es
        for e in self.ENG:
            for ins in self.q[e]:
                for d in ins.deps:
                    if (not d.is_dma) and self._needs_wait(ins, d):
                        d.sig = True
        for e in self.ENG:
            t = 0
            for ins in self.q[e]:
                if ins.sig and not ins.is_dma:
                    t += 1
                ins.tick = t
        engsem = {e: es.enter_context(nc.semaphore(f"sem_{e}")) for e in ("pe", "act", "dve", "pool")}
        nsem = 4
        for b in self.bufs.values():
            if b.ntick > 0:
                b.sem = es.enter_context(nc.semaphore(f"dsem{nsem}"))
                nsem += 1
        self.nsem = nsem
        block = es.enter_context(nc.Block())

        def run(ename, e):
            waited = {}
            mydma = {}
            for ins in self.q[ename]:
                need = {}
                for d in ins.deps:
                    if not self._needs_wait(ins, d):
                        continue
                    if d.is_dma:
                        key = ("dma", id(d.sembuf))
                        sem = d.sembuf.sem
                        val = d.dtick
                    else:
                        key = d.eng
                        sem = engsem[d.eng]
                        val = d.tick
                    if key not in need or need[key][1] < val:
                        need[key] = (sem, val)
                for key, (sem, val) in need.items():
                    if waited.get(key, 0) < val:
                        e.wait_ge(sem, val)
                        waited[key] = val
                bi = ins.fn(e)
                try:
                    self.where[str(bi.ins.name)] = ins.where
                except Exception:
                    try:
                        self.where[str(bi.name)] = ins.where
                    except Exception:
                        pass
                if ins.is_dma:
                    bi.then_inc(ins.sembuf.sem, 16)
                    mydma[id(ins.sembuf)] = (ins.sembuf.sem, ins.dtick)
                elif ins.sig:
                    bi.then_inc(engsem[ename], 1)
            for sem, val in mydma.values():
                e.wait_ge(sem, val)

        @block.sync
        def _(e):
            run("sp", e)

        @block.tensor
        def _(e):
            run("pe", e)

        @block.scalar
        def _(e):
            run("act", e)

        @block.vector
        def _(e):
            run("dve", e)

        @block.gpsimd
        def _(e):
            run("pool", e)

        es.close()
        return nc


def _colblocks(w, ncols=128):
    K, N = w.shape
    kc = K // 128
    nb = N // ncols
    return np.ascontiguousarray(w.reshape(kc, 128, nb, ncols).transpose(2, 1, 0, 3).reshape(nb, 128, kc * ncols))


def _pp(v):
    n = v.shape[0] // 128
    return np.ascontiguousarray(v.reshape(n, 128).T)


def _r(a, b):
    return list(range(a, b))


FM_CHUNKS = [
    ("AK", _r(256, 384)), ("BK0", _r(768, 896)), ("BK1", _r(896, 1024)), ("C0", _r(1280, 1408)), ("C1", _r(1408, 1536)),
    ("DK0", _r(1792, 1920)), ("DK1", _r(1920, 2048)),
    ("AQ0", _r(0, 64) + _r(128, 192)), ("AQ1", _r(64, 128) + _r(192, 256)), ("BQ0", _r(512, 640)), ("BQ1", _r(640, 768)),
    ("DQ0", _r(1536, 1664)), ("DQ1", _r(1664, 1792)), ("G0", _r(2304, 2432)), ("G1", _r(2432, 2560)),
]
FM_IDX = {n: i for i, (n, _) in enumerate(FM_CHUNKS)}
TM_COLS = _r(384, 512) + _r(1024, 1280) + _r(1792, 2048) + _r(2048, 2304)
WOUT_ROWS = _r(0, 64) + _r(128, 192) + _r(64, 128) + _r(192, 256) + _r(256, 1024)
POOL_WINDOWS = (2, 4, 8, 16)


def host_consts():
    c = {}
    n = NLAT
    rows = np.repeat(np.arange(n // 64), 64).astype(np.float32)
    cols = np.tile(np.arange(64), n // 64).astype(np.float32)

    def tables(dim):
        quarter = dim // 4
        inv = (np.float32(10000.0) ** (-np.arange(quarter, dtype=np.float32) / np.float32(quarter))).astype(np.float32)
        ang = np.concatenate([rows[:, None] * inv, cols[:, None] * inv], axis=-1).astype(np.float32)
        return np.cos(ang).astype(np.float32), np.sin(ang).astype(np.float32)

    cosA, sinA = tables(64)
    cosB, sinB = tables(32)
    p = np.arange(128)
    dA = p % 64
    idxA = (dA // 32) * 16 + (dA % 16)
    halfA = (dA % 32) // 16
    dB = p % 32
    idxB = (dB // 16) * 8 + (dB % 8)
    halfB = (dB % 16) // 8
    c["ropeA"] = np.ascontiguousarray(np.stack([cosA[:, idxA].T, sinA[:, idxA].T], axis=0))
    c["ropeB"] = np.ascontiguousarray(np.stack([cosB[:, idxB].T, sinB[:, idxB].T], axis=0))
    RA = np.zeros((128, 128), np.float32)
    RB = np.zeros((128, 128), np.float32)
    for m in range(128):
        pa = m + 16 if halfA[m] == 0 else m - 16
        RA[pa, m] = -1.0 if halfA[m] == 0 else 1.0
        pb = m + 8 if halfB[m] == 0 else m - 8
        RB[pb, m] = -1.0 if halfB[m] == 0 else 1.0
    c["perm"] = np.ascontiguousarray(np.stack([RA, RB], axis=1))
    mm = p[:, None].astype(np.float32)
    nn = p[None, :].astype(np.float32)
    cst = np.stack([np.maximum(nn - mm, 0), np.maximum(mm - nn, 0), np.eye(128, dtype=np.float32),
                    np.broadcast_to(nn + 1, (128, 128)), np.broadcast_to(128 - nn, (128, 128))], axis=1)
    c["cst"] = np.ascontiguousarray(cst.astype(np.float32))
    c["ccol"] = np.ascontiguousarray(np.stack([127 - p, p], axis=1).astype(np.float32))

    def invc(l):
        t = np.arange(l)
        out = np.zeros((128, 2, l), np.float32)
        for g, w in enumerate(POOL_WINDOWS):
            back = w // 2
            ahead = w - 1 - back
            cnt = (np.minimum(t + ahead, l - 1) - np.maximum(t - back, 0) + 1).astype(np.float32)
            ch, half = g // 2, g % 2
            out[half * 64:(half + 1) * 64, ch, :] = (np.float32(1.0) / cnt)[None, :]
        return out

    c["invc_lat"] = invc(NLAT)
    c["invc_ctx"] = invc(NCTX)
    return c


def host_prep(inputs):
    sh = dict(host_consts())
    w_mod = inputs["w_mod"]
    for i in range(DEPTH):
        sh[f"wm{i}"] = _colblocks(w_mod[i])
        bm = _pp(inputs["b_mod"][i])
        sh[f"bm{i}"] = np.ascontiguousarray(np.repeat(bm[:, :, None], 2, axis=2))
        ng = inputs["norm_gain"][i]
        ngp = np.stack([_pp(ng[j]) for j in range(6)], axis=1)
        sh[f"ng{i}"] = np.ascontiguousarray(np.repeat(ngp[:, :, :, None], 2, axis=3))
        for s in range(2):
            sh[f"wg{i}{s}"] = _colblocks(inputs["ffn_w_gate"][i, s])
            sh[f"wu{i}{s}"] = _colblocks(inputs["ffn_w_up"][i, s])
            sh[f"wd{i}{s}"] = _colblocks(inputs["ffn_w_down"][i, s])
        w_in = inputs["w_in"][i]
        sh[f"winf{i}"] = np.ascontiguousarray(np.stack([_colblocks(w_in[:, cols])[0] for _, cols in FM_CHUNKS], axis=0))
        wt = w_in[:, TM_COLS].reshape(8, 128, 896).transpose(1, 0, 2).reshape(128, 7, 1024)
        sh[f"wt{i}"] = np.ascontiguousarray(wt.transpose(1, 0, 2))
        sh[f"wo{i}"] = _colblocks(inputs["w_out"][i][WOUT_ROWS, :])
        pv = np.zeros((128, 8), np.float32)
        pv[:, 0] = np.tile(inputs["attn_qk_gain"][i, 0], 2)
        pv[:, 1] = np.tile(inputs["attn_qk_gain"][i, 1], 2)
        pv[:, 2] = np.tile(inputs["diff_out_gain"][i], 2)
        pv[:, 3] = np.tile(inputs["ret_out_gain"][i], 2)
        pv[:, 4:6] = _pp(inputs["pool_scale"][i])
        sh[f"pv{i}"] = pv
        pw = np.zeros((128, 2, 128), np.float32)
        for g in range(4):
            ch, half = g // 2, g % 2
            pw[half * 64:(half + 1) * 64, ch, half * 64:(half + 1) * 64] = inputs["pool_w"][i, g]
        sh[f"pw{i}"] = pw
        sh[f"rdl{i}"] = np.ascontiguousarray(np.broadcast_to(inputs["ret_decay_logit"][i].reshape(1, 8), (128, 8)))
        sh[f"dl{i}"] = np.ascontiguousarray(np.broadcast_to(inputs["diff_lambda"][i].reshape(1, 128), (128, 128)))
    per = []
    B = inputs["x"].shape[0]
    for b in range(B):
        d = {}
        xt = np.concatenate([inputs["ctx"][b].T, inputs["x"][b].T], axis=1)
        d["xT"] = np.ascontiguousarray(xt.reshape(8, 128, NTOK))
        cc = np.stack([inputs["c"][b], inputs["c_ctx"]], axis=1)
        d["cT"] = np.ascontiguousarray(cc.reshape(8, 128, 2).transpose(1, 0, 2))
        per.append(d)
    return sh, per


LAT, CTX = 0, 1
SCB = (0, 1, 2, 3, 6, 7)
NT = NTOK // 128


class _Stop(Exception):
    pass


class Prog:
    def chk(self, tag):
        if getattr(self, "kv_stop", None) == tag:
            raise _Stop()

    def __init__(self, shapes):
        self.k = KB()
        k = self.k

        class _Lazy(dict):
            def __missing__(d, name):
                shape, dt = shapes[name]
                d[name] = k.dram(name, shape, dt, "ExternalInput")
                return d[name]

        self.inp = _Lazy()
        self.out = k.dram("out", [8, 128, NLAT], F32, "ExternalOutput")
        self.hs = k.dram("hs", [8, 128, NTOK], F32, "Internal")
        k.psum_banks()
        self.cnt = {}
        self.alloc()

    def alloc(self):
        k = self.k
        sb = k.sbuf
        self.ones = sb("ones", [128, 128], BF16)
        self.BO = sb("BO", [128, 128], BF16)
        self.permf = sb("permf", [128, 2, 128], F32)
        self.perm = sb("perm", [128, 2, 128], BF16)
        self.cst = sb("cst", [128, 5, 128], F32)
        self.ccol = sb("ccol", [128, 2], F32)
        self.cT = sb("cT", [128, 8, 2], F32)
        self.cTb = sb("cTb", [128, 8, 2], BF16)
        self.bm = sb("bm", [128, 72, 2], F32)
        self.ng = sb("ng", [128, 6, 8, 2], F32)
        self.modT_l = [sb(f"modT{i}", [128, 3, 3, 8, 2], F32) for i in range(DEPTH)]
        self.mA_l = [sb(f"mA{i}", [128, 3, 8, 2], F32) for i in range(DEPTH)]
        self.mG_l = [sb(f"mG{i}", [128, 3, 8, 2], F32) for i in range(DEPTH)]
        self.epsc = sb("epsc", [128, 1], F32)
        self.one1 = sb("one1", [128, 1], F32)
        self.pv = sb("pv", [128, 8], F32)
        self.pwf = sb("pwf", [128, 2, 128], F32)
        self.pwb = sb("pwb", [128, 2, 128], BF16)
        self.rdl = sb("rdl", [128, 8], F32)
        self.lg = sb("lg", [128, 8], F32)
        self.lgpp = sb("lgpp", [128, 4], F32)
        self.g128 = sb("g128", [128, 4], F32)
        self.dl = sb("dl", [128, 128], F32)
        self.lamt = sb("lamt", [128, 4], F32)
        self.neglam = sb("neglam", [128, 1], F32)
        self.tabq = sb("tabq", [128, 2, 2, 128], F32)
        self.tabk = sb("tabk", [128, 2, 4], F32)
        self.mask = sb("mask", [128, 4, 128], F32)
        self.mtmp = sb("mtmp", [128, 128], F32)
        ARENA = 196096
        self.arena = sb("arena", [128, ARENA // 4], F32)
        self.atop = 0

        def view(nbytes, dt, pat=None, **kw):
            off = self.atop
            assert off % 4 == 0 and nbytes % 4 == 0
            self.atop += nbytes
            assert self.atop <= ARENA, (self.atop, ARENA)
            v = self.arena[:, off // 4:(off + nbytes) // 4]
            if dt is BF16:
                v = v.bitcast(BF16)
            if pat is not None:
                v = v.rearrange(pat, **kw)
            return v

        self.hm = view(8192, BF16, "p (a b) -> p a b", b=512)
        self.ff = view(16384, F32, "p (a b) -> p a b", b=512)
        self.t1 = [view(2048, F32) for _ in range(2)]
        self.sg = [view(2048, F32) for _ in range(2)]
        self.rstd = view(2048, F32)
        self.wst = [view(4096, F32) for _ in range(4)]
        self.wbf = [view(2048, BF16, "p (a b) -> p a b", b=128) for _ in range(4)]
        mark = self.atop
        self.hb = [view(16384, F32, "p (a b) -> p a b", b=512) for _ in range(2)]
        self.av = view(22528, BF16, "p (a b) -> p a b", b=512)
        self.wdst = [view(5632, F32) for _ in range(2)]
        self.wdbf = [view(5632, BF16, "p (a b) -> p a b", b=128) for _ in range(2)]
        self.hm2 = view(8192, BF16, "p (a b) -> p a b", b=512)
        self.mst = [view(4096, F32) for _ in range(2)]
        self.mbf = [view(2048, BF16, "p (a b) -> p a b", b=128) for _ in range(2)]
        self.sq2 = view(8192, BF16, "p (a b) -> p a b", b=512)
        self.rstd2 = view(2048, F32)
        ffn_top = self.atop
        self.atop = mark
        self.kA = view(4608, BF16)
        self.vAf = view(9216, BF16)
        self.vA = self.vAf.rearrange("p (t h e) -> p t h e", t=NT, h=2)
        self.kB = view(9216, BF16, "p (j t) -> p j t", j=2)
        self.vBf = view(18432, BF16)
        self.vB = self.vBf.rearrange("p (t h e) -> p t h e", t=NT, h=4)
        self.kD = view(9216, BF16, "p (j t) -> p j t", j=2)
        self.vD = view(9216, BF16, "p (t h e) -> p t h e", t=NT, h=4)
        self.KVB = view(9216, F32, "p (t j e) -> p t j e", t=NT, j=2)
        self.SfB = view(4608, BF16, "p (t j e) -> p t j e", t=NT, j=2)
        self.SbB = view(4608, BF16, "p (t j e) -> p t j e", t=NT, j=2)
        self.yC = view(9216, BF16, "p (j t) -> p j t", j=2)
        self.rope = view(8192, F32, "p (g c t) -> p g c t", g=2, c=2)
        mark2 = self.atop
        self.uC = view(9216, BF16, "p (j t) -> p j t", j=2)
        self.wtbf = view(14336, BF16, "p (a b) -> p a b", b=896)
        self.ptmp = [view(1216, F32) for _ in range(4)]
        self.ktmp = view(1024, BF16, "p (d h e) -> p d h e", d=2, h=4)
        self.S32f = view(1024, F32)
        self.S32 = self.S32f.rearrange("p (d j e) -> p d j e", d=2, j=2)
        self.invb = view(2048, F32, "p (j t) -> p j t", j=2)
        self.poolbs = [view(512, BF16) for _ in range(6)]
        self.kvsq = [view(1024, BF16) for _ in range(2)]
        kv_top = self.atop
        self.atop = mark2
        self.qA = view(2048, BF16, "p (j t) -> p j t", j=2)
        self.qB = view(2048, BF16, "p (j t) -> p j t", j=2)
        self.qD = view(2048, BF16, "p (j t) -> p j t", j=2)
        self.qfb = view(4096, BF16, "p (d j t) -> p d j t", d=2, j=2)
        self.gate = view(4096, F32, "p (j t) -> p j t", j=2)
        self.yblk = view(6144, BF16, "p (j t) -> p j t", j=6)
        self.PT = [view(1024, BF16) for _ in range(6)]
        self.ft = [view(2048, F32) for _ in range(4)]
        self.ydt = view(2048, F32)
        self.hc = [view(2048, F32) for _ in range(2)]
        self.arena_used = max(ffn_top, kv_top, self.atop)

    def rr(self, name, n):
        c = self.cnt.get(name, 0)
        self.cnt[name] = c + 1
        return c % n

    def prologue(self):
        k = self.k
        B = k.buf
        k.op("pool", I.memset(self.ones[:], 1.0), w=[B("ones")])
        k.op("pool", I.memset(self.epsc[:], EPS), w=[B("epsc")])
        k.op("pool", I.memset(self.one1[:], 1.0), w=[B("one1")])
        k.op("pool", I.memset(self.BO[:], 0.0), w=[B("BO")])
        k.op("pool", I.memset(self.BO[0:64, 0:64], 1.0), w=[B("BO")])
        k.op("pool", I.memset(self.BO[64:128, 64:128], 1.0), w=[B("BO")])
        k.dma("sp", self.cT[:], self.inp["cT"], w=[B("cT")])
        k.dma("sp", self.permf[:], self.inp["perm"], w=[B("permf")])
        k.dma("sp", self.cst[:], self.inp["cst"], w=[B("cst")])
        k.dma("sp", self.ccol[:], self.inp["ccol"], w=[B("ccol")])
        k.op("dve", I.tensor_copy(out=self.perm[:], in_=self.permf[:]), r=[B("permf")], w=[B("perm")])
        k.op("act", I.activation(out=self.cTb[:], in_=self.cT[:], func=AF.Silu), r=[B("cT")], w=[B("cTb")])

    def setlay(self, lay):
        self.modT, self.mA, self.mG = self.modT_l[lay], self.mA_l[lay], self.mG_l[lay]
        self.L = lay

    def mB(self, name):
        return self.k.buf(name, self.L)

    def mod(self, lay):
        for _ in self.mod_gen(lay, own=False):
            pass

    def mod_tick(self):
        g = getattr(self, "modgen", None)
        if g is None:
            return False
        try:
            next(g)
            return True
        except StopIteration:
            self.modgen = None
            return False

    def mod_gen(self, lay, own):
        k = self.k
        modT, mA, mG = self.modT_l[lay], self.mA_l[lay], self.mG_l[lay]
        B = lambda *a: (k.buf(a[0], lay) if a[0] in ("modT", "mA", "mG") else k.buf(*a))
        k.dma("sp", self.bm[:], self.inp[f"bm{lay}"], w=[B("bm")])
        k.dma("sp", self.ng[:], self.inp[f"ng{lay}"], w=[B("ng")])
        wm = self.inp[f"wm{lay}"]
        bank, bb = k.banks[7]
        if not own:
            self.plan_w([((("wm", lay), cb, 72), wm[cb]) for cb in range(72)])

        def issue(cb):
            s = cb % 2
            k.dma("sp", self.mst[s], wm[cb], w=[B("mst", s)])
            k.op("act", I.activation(out=self.mbf[s].rearrange("p a b -> p (a b)"), in_=self.mst[s], func=AF.Copy), r=[B("mst", s)], w=[B("mbf", s)])

        if own:
            issue(0)
        for cb in range(72):
            if own:
                wt, wb = self.mbf[cb % 2], B("mbf", cb % 2)
            else:
                s = self.load_w()
                wt, wb = self.wbf[s], B("wbf", s)
            for kc in range(8):
                k.op("pe", I.matmul(bank[:, 2 * cb:2 * cb + 2], wt[:, kc, :], self.cTb[:, kc, :], start=(kc == 0), stop=(kc == 7)),
                     r=[wb, B("cTb")], w=[bb])
            if own and cb + 1 < 72:
                issue(cb + 1)
            yield cb
        k.op("dve", I.tensor_tensor(out=modT[:].rearrange("p a b c d -> p (a b c d)"), in0=bank[:, 0:144],
                                              in1=self.bm[:].rearrange("p a b -> p (a b)"), op=ALU.add),
             r=[bb, B("bm")], w=[B("modT")])
        for sub in range(3):
            coef = 1.0 if sub == 1 else 0.5
            k.op("dve", I.scalar_tensor_tensor(out=mA[:, sub], in0=modT[:, sub, 1], scalar=1.0,
                                                                in1=self.ng[:, 2 * sub], op0=ALU.add, op1=ALU.mult),
                 r=[B("modT"), B("ng")], w=[B("mA")])
            k.op("dve", I.scalar_tensor_tensor(out=mG[:, sub], in0=modT[:, sub, 2], scalar=coef,
                                                                          in1=self.ng[:, 2 * sub + 1], op0=ALU.mult, op1=ALU.mult),
                 r=[B("modT"), B("ng")], w=[B("mG")])

    def rstd_from_ssq(self, bank, bb, T, nfeat, out_ap, out_buf):
        k = self.k
        B = k.buf
        k.op("act", I.activation(out=out_ap, in_=bank[:, :T], func=AF.Ln, scale=1.0 / nfeat, bias=self.epsc[:]),
             r=[bb, B("epsc")], w=[out_buf])
        k.op("act", I.activation(out=out_ap, in_=out_ap, func=AF.Exp, scale=-0.5), r=[out_buf], w=[out_buf])

    def rms_rstd(self, src, srcbuf, T, hm=None, hmbuf=None, part=None):
        k = self.k
        B = k.buf
        hm = self.hm if hm is None else hm
        hmbuf = B("hm") if hmbuf is None else hmbuf
        bank, bb = k.banks[6]
        if part in (None, "a"):
            k.op("act", I.activation(out=hm[:, :, :T], in_=src, func=AF.Square), r=[srcbuf], w=[hmbuf])
        if part == "a":
            return
        for kc in range(8):
            k.op("pe", I.matmul(bank[:, :T], self.ones[:], hm[:, kc, :T], start=(kc == 0), stop=(kc == 7)),
                 r=[hmbuf, B("ones")], w=[bb])
        self.rstd_from_ssq(bank, bb, T, D, self.rstd[:, :T], B("rstd"))

    def modulate(self, hb, hbuf, T, sub, kind, hm=None, hmbuf=None, part=None):
        k = self.k
        B = k.buf
        mA, modT = self.mA, self.modT
        hm = self.hm if hm is None else hm
        hmbuf = B("hm") if hmbuf is None else hmbuf
        self.rms_rstd(hb[:, :, :T], hbuf, T, hm, hmbuf, part)
        if part == "a":
            return
        for kc in range(8):
            s = self.rr("t1", 2)
            k.op("dve", I.tensor_tensor(out=self.t1[s][:, :T], in0=hb[:, kc, :T], in1=self.rstd[:, :T], op=ALU.mult),
                 r=[hbuf, B("rstd")], w=[B("t1", s)])
            k.op("act", I.activation(out=hm[:, kc, :T], in_=self.t1[s][:, :T], func=AF.Identity,
                                                         scale=mA[:, sub, kc, kind:kind + 1], bias=modT[:, sub, 0, kc, kind:kind + 1]),
                 r=[B("t1", s), self.mB("mA"), self.mB("modT")], w=[hmbuf])

    def plan_w(self, items, depth=2):
        self.wplan = list(items)
        self.wpos = 0
        self.wiss = 0
        self.wslot = {}
        self.wdepth = depth

    def _cache(self, group, n, width):
        if not hasattr(self, "wcache"):
            self.wcache = {}
            self.wcached = set()
        if group not in self.wcache:
            name = "wc_" + "_".join(str(x) for x in group)
            self.wcache[group] = self.k.dram(name, [n, 128, width], BF16, "Internal")
        return self.wcache[group], self.k.buf("wcache", group)

    def _issue_w(self, i):
        k = self.k
        B = k.buf
        (group, idx, n), src = self.wplan[i]
        s = self.rr("wst", 4)
        once = group[0] == "wm"
        if not once:
            cache, cbuf = self._cache(group, n, 1024)
        flat = self.wbf[s].rearrange("p a b -> p (a b)")
        if (not once) and (group, idx) in self.wcached:
            k.dma("sp", flat, cache[idx], r=[cbuf], w=[B("wbf", s)])
        else:
            k.dma("sp", self.wst[s], src, w=[B("wst", s)])
            k.op("act", I.activation(out=flat, in_=self.wst[s], func=AF.Copy), r=[B("wst", s)], w=[B("wbf", s)])
            if not once:
                k.dma("act", cache[idx], flat, r=[B("wbf", s)], w=[cbuf], sembuf=B("wbf", s))
                self.wcached.add((group, idx))
        self.wslot[i] = s

    def load_w(self, src_ap=None):
        i = self.wpos
        self.wpos += 1
        while self.wiss <= min(i + self.wdepth, len(self.wplan) - 1):
            self._issue_w(self.wiss)
            self.wiss += 1
        return self.wslot.pop(i)

    def load_wd(self, wd, oc, group):
        k = self.k
        B = k.buf
        s = self.rr("wd", 2)
        cache, cbuf = self._cache(group, 8, 2816)
        flat = self.wdbf[s].rearrange("p a b -> p (a b)")
        if (group, oc) in self.wcached:
            k.dma("sp", flat, cache[oc], r=[cbuf], w=[B("wdbf", s)])
            return s
        for half in range(2):
            s1 = self.rr("wdst", 2)
            k.dma("sp", self.wdst[s1], wd[oc][:, half * 1408:(half + 1) * 1408], w=[B("wdst", s1)])
            k.op("act", I.activation(out=self.wdbf[s][:, half * 11:(half + 1) * 11, :].rearrange("p a b -> p (a b)"), in_=self.wdst[s1], func=AF.Copy),
                 r=[B("wdst", s1)], w=[B("wdbf", s)])
        k.dma("act", cache[oc], flat, r=[B("wdbf", s)], w=[cbuf], sembuf=B("wdbf", s))
        self.wcached.add((group, oc))
        return s

    def ffn_p2(self, T, lay, widx, hm, hmbuf, mid_hook=None):
        k = self.k
        B = k.buf
        wd = self.inp[f"wd{lay}{widx}"]
        wds = None
        for fc in range(NFC):
            if fc == 2 and mid_hook is not None:
                mid_hook("a")
            if fc == 7 and mid_hook is not None:
                mid_hook("b")
            sl = [self.load_w(), self.load_w()]
            pb = self.rr("gu", 2)
            (bg, bgb), (bu, bub) = k.banks[2 * pb], k.banks[2 * pb + 1]
            for (bank, bbuf), s in ((k.banks[2 * pb], sl[0]), (k.banks[2 * pb + 1], sl[1])):
                for kc in range(8):
                    k.op("pe", I.matmul(bank[:, :T], self.wbf[s][:, kc, :], hm[:, kc, :T], start=(kc == 0), stop=(kc == 7)),
                         r=[B("wbf", s), hmbuf], w=[bbuf])
            s2 = self.rr("sg", 2)
            k.op("act", I.activation(out=self.sg[s2][:, :T], in_=bg[:, :T], func=AF.Silu), r=[bgb], w=[B("sg", s2)])
            k.op("dve", I.tensor_tensor(out=self.av[:, fc, :T], in0=bu[:, :T], in1=self.sg[s2][:, :T], op=ALU.mult),
                 r=[bub, B("sg", s2)], w=[B("av")])
            if fc == NFC - 3:
                wds = self.load_wd(wd, 0, ("wd", lay, widx))
            self.mod_tick()
        return wds

    def ffn_p3(self, T, lay, widx, wds):
        k = self.k
        B = k.buf
        wd = self.inp[f"wd{lay}{widx}"]
        b6, bb6 = k.banks[6]
        pend = None
        for oc in range(8):
            s = wds
            if oc + 1 < 8:
                wds = self.load_wd(wd, oc + 1, ("wd", lay, widx))
            bank, bbuf = k.banks[4 + self.rr("dn", 2)]
            for fc in range(NFC):
                k.op("pe", I.matmul(bank[:, :T], self.wdbf[s][:, fc, :], self.av[:, fc, :T], start=(fc == 0), stop=(fc == NFC - 1)),
                     r=[B("wdbf", s), B("av")], w=[bbuf])
            if pend is not None:
                pend()
            k.op("dve", I.tensor_copy(out=self.ff[:, oc, :T], in_=bank[:, :T]), r=[bbuf], w=[B("ff", oc)])
            k.op("act", I.activation(out=self.sq2[:, oc, :T], in_=self.ff[:, oc, :T], func=AF.Square), r=[B("ff", oc)], w=[B("sq2", oc)])

            def ssq(oc=oc):
                k.op("pe", I.matmul(b6[:, :T], self.ones[:], self.sq2[:, oc, :T], start=(oc == 0), stop=(oc == 7)),
                     r=[B("sq2", oc), B("ones")], w=[bb6])
            pend = ssq
        pend()

    def ffn_p4(self, hb, hbuf, T, kind, lay, sub):
        k = self.k
        B = k.buf
        self.setlay(lay)
        mG = self.mG
        b6, bb6 = k.banks[6]
        self.rstd_from_ssq(b6, bb6, T, D, self.rstd2[:, :T], B("rstd2"))
        for kc in range(8):
            s = self.rr("t1", 2)
            k.op("dve", I.scalar_tensor_tensor(out=self.t1[s][:, :T], in0=self.ff[:, kc, :T],
                                                                    scalar=mG[:, sub, kc, kind:kind + 1], in1=self.rstd2[:, :T],
                                                                    op0=ALU.mult, op1=ALU.mult),
                 r=[B("ff", kc), self.mB("mG"), B("rstd2")], w=[B("t1", s)])
            k.op("pool", I.tensor_tensor(out=hb[:, kc, :T], in0=hb[:, kc, :T], in1=self.t1[s][:, :T], op=ALU.add),
                 r=[B("t1", s), hbuf], w=[hbuf])

    def blocks(self, with_ctx=True):
        bl = []
        if with_ctx:
            bl.append((0, NCTX, CTX))
        for j in range(NLAT // 512):
            bl.append((NCTX + 512 * j, 512, LAT))
        return bl

    def stage_F(self, src, post_of=None, pre_of=None, with_ctx=True, dst=None):
        k = self.k
        B = k.buf
        blocks = self.blocks(with_ctx)
        if with_ctx:
            blocks = blocks[1:] + blocks[:1]
        ffns = ([(post_of, 2, 1)] if post_of is not None else []) + ([(pre_of, 0, 0)] if pre_of is not None else [])
        tasks = []
        if len(ffns) == 1:
            tasks = [(bi, 0) for bi in range(len(blocks))]
        else:
            for p in range(0, len(blocks), 2):
                grp = list(range(p, min(p + 2, len(blocks))))
                for fi in range(2):
                    tasks += [(bi, fi) for bi in grp]
        items = []
        for (bi, fi) in tasks:
            ly, sub, wi = ffns[fi]
            for fc in range(NFC):
                items += [((("wg", ly, wi), fc, NFC), self.inp[f"wg{ly}{wi}"][fc]), ((("wu", ly, wi), fc, NFC), self.inp[f"wu{ly}{wi}"][fc])]
        self.plan_w(items)

        def hbof(bi):
            return self.hb[bi % 2], B("hb", bi % 2)

        def load(bi):
            t0, T, kind = blocks[bi]
            hb, hbuf = hbof(bi)
            k.dma("sp", hb[:, :, :T], src[:, :, t0:t0 + T].rearrange("c p t -> p c t"), r=[B("hs", t0)] if src is self.hs else [], w=[hbuf])

        def store(bi):
            t0, T, kind = blocks[bi]
            hb, hbuf = hbof(bi)
            if dst is None:
                k.dma("sp", self.hs[:, :, t0:t0 + T].rearrange("c p t -> p c t"), hb[:, :, :T], r=[hbuf], w=[B("hs", t0)], sembuf=hbuf)
            else:
                k.dma("sp", dst[:, :, t0 - NCTX:t0 - NCTX + T].rearrange("c p t -> p c t"), hb[:, :, :T], r=[hbuf], w=[B("outd")], sembuf=hbuf)

        def p1(ti, part=None):
            bi, fi = tasks[ti]
            t0, T, kind = blocks[bi]
            ly, sub, wi = ffns[fi]
            if fi == 0 and part in (None, "a"):
                load(bi)
            hb, hbuf = hbof(bi)
            self.setlay(ly)
            self.modulate(hb, hbuf, T, sub, kind, hms[ti % 2][0], hms[ti % 2][1], part)

        hms = [(self.hm, B("hm")), (self.hm2, B("hm2"))]
        p1(0)
        for ti, (bi, fi) in enumerate(tasks):
            t0, T, kind = blocks[bi]
            ly, sub, wi = ffns[fi]
            hb, hbuf = hbof(bi)
            hoist = ti + 1 < len(tasks) and tasks[ti + 1][0] != bi
            wds = self.ffn_p2(T, ly, wi, hms[ti % 2][0], hms[ti % 2][1], (lambda part, ti=ti: p1(ti + 1, part)) if hoist else None)
            self.ffn_p3(T, ly, wi, wds)
            self.ffn_p4(hb, hbuf, T, kind, ly, sub)
            if fi == len(ffns) - 1:
                store(bi)
            if ti + 1 < len(tasks) and not hoist:
                p1(ti + 1)

    def prep_mixer(self, lay):
        k = self.k
        B = k.buf
        k.dma("sp", self.pv[:], self.inp[f"pv{lay}"], w=[B("pv")])
        k.dma("sp", self.pwf[:], self.inp[f"pw{lay}"], w=[B("pwf")])
        k.dma("sp", self.rdl[:], self.inp[f"rdl{lay}"], w=[B("rdl")])
        k.dma("sp", self.dl[:], self.inp[f"dl{lay}"], w=[B("dl")])
        k.op("dve", I.tensor_copy(out=self.pwb[:], in_=self.pwf[:]), r=[B("pwf")], w=[B("pwb")])
        k.op("act", I.activation(out=self.lg[:], in_=self.rdl[:], func=AF.Exp, scale=-1.0), r=[B("rdl")], w=[B("lg")])
        k.op("act", I.activation(out=self.lg[:], in_=self.lg[:], func=AF.Ln, bias=self.one1[:]), r=[B("lg"), B("one1")], w=[B("lg")])
        k.op("dve", I.tensor_scalar(out=self.lg[:], in0=self.lg[:], scalar1=-1.0, scalar2=None, op0=ALU.mult), r=[B("lg")], w=[B("lg")])
        for d in range(2):
            for j in range(2):
                for half in range(2):
                    pr = slice(half * 64, half * 64 + 64)
                    src = d * 4 + 2 * j + half
                    k.op("dve", I.tensor_copy(out=self.lgpp[pr, d * 2 + j:d * 2 + j + 1], in_=self.lg[pr, src:src + 1]),
                         r=[B("lg")], w=[B("lgpp")])
        k.op("act", I.activation(out=self.g128[:], in_=self.lgpp[:], func=AF.Exp, scale=128.0), r=[B("lgpp")], w=[B("g128")])
        for d in range(2):
            for j in range(2):
                k.op("act", I.activation(out=self.tabq[:, d, j, :], in_=self.cst[:, 3 + d, :], func=AF.Exp,
                                                           scale=self.lgpp[:, d * 2 + j:d * 2 + j + 1]),
                     r=[B("cst"), B("lgpp")], w=[B("tabq")])
        for d in range(2):
            k.op("act", I.activation(out=self.tabk[:, d, :], in_=self.lg[:, d * 4:d * 4 + 4], func=AF.Exp, scale=self.ccol[:, d:d + 1]),
                 r=[B("lg"), B("ccol")], w=[B("tabk")])
        k.op("dve", I.tensor_scalar(out=self.tabk[:], in0=self.tabk[:], scalar1=0.125, scalar2=None, op0=ALU.mult), r=[B("tabk")], w=[B("tabk")])
        for h in range(4):
            k.op("act", I.activation(out=self.mask[:, h, :], in_=self.cst[:, 0, :], func=AF.Exp, scale=self.lg[:, h:h + 1]),
                 r=[B("cst"), B("lg")], w=[B("mask")])
            k.op("act", I.activation(out=self.mtmp[:], in_=self.cst[:, 1, :], func=AF.Exp, scale=self.lg[:, 4 + h:5 + h]),
                 r=[B("cst"), B("lg")], w=[B("mtmp")])
            k.op("dve", I.tensor_tensor(out=self.mask[:, h, :], in0=self.mask[:, h, :], in1=self.mtmp[:], op=ALU.mult),
                 r=[B("mask"), B("mtmp")], w=[B("mask")])
            k.op("dve", I.tensor_tensor(out=self.mask[:, h, :], in0=self.mask[:, h, :], in1=self.cst[:, 2, :], op=ALU.add),
                 r=[B("mask"), B("cst")], w=[B("mask")])
        k.op("dve", I.tensor_scalar(out=self.mask[:], in0=self.mask[:], scalar1=0.125, scalar2=None, op0=ALU.mult), r=[B("mask")], w=[B("mask")])
        lam_init = 0.8 - 0.6 * math.exp(-0.3 * lay)
        for i in range(2):
            k.op("dve", I.tensor_tensor(out=self.mtmp[:, i * 32:(i + 1) * 32], in0=self.dl[:, 64 * i:64 * i + 32],
                                                       in1=self.dl[:, 64 * i + 32:64 * i + 64], op=ALU.mult),
                 r=[B("dl")], w=[B("mtmp")])
            k.op("dve", I.tensor_reduce(out=self.lamt[:, i:i + 1], in_=self.mtmp[:, i * 32:(i + 1) * 32], axis=mybir.AxisListType.X, op=ALU.add),
                 r=[B("mtmp")], w=[B("lamt")])
        k.op("act", I.activation(out=self.lamt[:, 2:4], in_=self.lamt[:, 0:2], func=AF.Exp), r=[B("lamt")], w=[B("lamt")])
        k.op("dve", I.scalar_tensor_tensor(out=self.neglam[:], in0=self.lamt[:, 3:4], scalar=-lam_init, in1=self.lamt[:, 2:3],
                                                     op0=ALU.add, op1=ALU.subtract),
             r=[B("lamt")], w=[B("neglam")])
        self.lam_init = lam_init

    def load_rope(self, t0, T):
        k = self.k
        B = k.buf
        l0 = t0 - NCTX
        k.dma("sp", self.rope[:, 0, :, :T], self.inp["ropeA"][:, :, l0:l0 + T].rearrange("c p t -> p c t"), w=[B("rope")])
        k.dma("sp", self.rope[:, 1, :, :T], self.inp["ropeB"][:, :, l0:l0 + T].rearrange("c p t -> p c t"), w=[B("rope")])

    def proj_fm(self, lay, name, T, bank_i):
        k = self.k
        B = k.buf
        s = self.load_w(self.inp[f"winf{lay}"][FM_IDX[name]])
        bank, bb = k.banks[bank_i]
        for kc in range(8):
            k.op("pe", I.matmul(bank[:, :T], self.wbf[s][:, kc, :], self.hm[:, kc, :T], start=(kc == 0), stop=(kc == 7)),
                 r=[B("wbf", s), B("hm")], w=[bb])
        return bank, bb

    def headnorm(self, src_ap, src_bufs, T, gain_ap, gain_bufs, out_ap, out_buf, imm=1.0, sq_eng_in_psum=True):
        k = self.k
        B = k.buf
        bank, bb = k.banks[6]
        sqb = self.PTs()
        k.op("act", I.activation(out=sqb[0][:, :T], in_=src_ap, func=AF.Square), r=src_bufs, w=[sqb[1]])
        k.op("pe", I.matmul(bank[:, :T], self.BO[:], sqb[0][:, :T], start=True, stop=True), r=[sqb[1], B("BO")], w=[bb])
        self.rstd_from_ssq(bank, bb, T, 64, self.rstd[:, :T], B("rstd"))
        if imm != 1.0:
            k.op("dve", I.tensor_scalar(out=self.rstd[:, :T], in0=self.rstd[:, :T], scalar1=float(imm), scalar2=None, op0=ALU.mult),
                 r=[B("rstd")], w=[B("rstd")])
        k.op("dve", I.scalar_tensor_tensor(out=out_ap, in0=src_ap, scalar=gain_ap, in1=self.rstd[:, :T], op0=ALU.mult, op1=ALU.mult),
             r=list(src_bufs) + list(gain_bufs) + [B("rstd")], w=[out_buf])

    WARM_N = 0

    def warm(self):
        k = self.k
        b7, bb7 = k.banks[7]
        for _ in range(self.WARM_N):
            k.op("pe", I.matmul(b7[:, 0:128], self.ones[:], self.ones[:], start=True, stop=True), r=[k.buf("ones")], w=[bb7])

    def PTs(self):
        if self.stage == "KV":
            s = self.rr("kvsq", 2)
            return (self.kvsq[s], self.k.buf("kvsq", s))
        s = self.rr("PT", 6)
        return (self.PT[s], self.k.buf("PT", s))

    def rope_apply(self, xb_ap, xb_buf, T, g, out_ap, out_buf):
        k = self.k
        B = k.buf
        bank, bb = k.banks[7]
        k.op("pe", I.matmul(bank[:, :T], self.perm[:, g, :], xb_ap, start=True, stop=True), r=[xb_buf, B("perm")], w=[bb])
        f0, f1 = self.fts(), self.fts()
        k.op("dve", I.tensor_tensor(out=f0[0][:, :T], in0=xb_ap, in1=self.rope[:, g, 0, :T], op=ALU.mult),
             r=[xb_buf, B("rope")], w=[f0[1]])
        k.op("dve", I.tensor_tensor(out=f1[0][:, :T], in0=bank[:, :T], in1=self.rope[:, g, 1, :T], op=ALU.mult),
             r=[bb, B("rope")], w=[f1[1]])
        k.op("pool", I.tensor_tensor(out=out_ap, in0=f0[0][:, :T], in1=f1[0][:, :T], op=ALU.add), r=[f0[1], f1[1]], w=[out_buf])

    def fts(self):
        if self.stage == "KV":
            s = self.rr("kvft", 4)
            return self.kvft[s]
        s = self.rr("ft", 4)
        return (self.ft[s], self.k.buf("ft", s))

    def stage_KV(self, lay):
        k = self.k
        B = k.buf
        self.stage = "KV"
        self.setlay(lay)
        self.kvft = [(self.t1[0], B("t1", 0)), (self.t1[1], B("t1", 1)), (self.sg[0], B("sg", 0)), (self.sg[1], B("sg", 1))]
        hbm, hbuf = self.ff, B("ff")
        for i in range(7):
            s = self.rr("wst", 4)
            k.dma("sp", self.wst[s], self.inp[f"wt{lay}"][i], w=[B("wst", s)])
            k.op("dve", I.tensor_copy(out=self.wtbf.rearrange("p a b -> p (a b)")[:, i * 1024:(i + 1) * 1024], in_=self.wst[s]),
                 r=[B("wst", s)], w=[B("wtbf")])
        k.op("pool", I.memset(self.vAf, 1.0), w=[B("vA")])
        k.op("pool", I.memset(self.vBf, 1.0), w=[B("vB")])
        k.op("pool", I.memset(self.S32f, 0.0), w=[B("S32")])
        for i_ in range(4):
            k.op("pool", I.memset(self.ptmp[i_], 0.0), w=[B("ptmp", i_)])
        self.pool_pending = []
        wf = self.inp[f"winf{lay}"]
        self.plan_w([((("winf", lay), FM_IDX[n], 15), wf[FM_IDX[n]]) for _ in self.blocks(True) for n in ("AK", "BK0", "BK1", "C0", "C1", "DK0", "DK1")])
        for (t0, T, kind) in self.blocks(True):
            k.dma("sp", hbm[:, :, :T], self.hs[:, :, t0:t0 + T].rearrange("c p t -> p c t"), r=[B("hs", t0)], w=[hbuf])
            self.modulate(hbm, hbuf, T, 1, kind)
            if kind == LAT:
                self.load_rope(t0, T)
                self.chk("KVl")
            bank, bb = self.proj_fm(lay, "AK", T, self.rr("pj", 2))
            if kind == CTX:
                self.headnorm(bank[:, :T], [bb], T, self.pv[:, 1:2], [B("pv")], self.kA[:, t0:t0 + T], B("kA"))
            else:
                xb = self.PTs()
                self.headnorm(bank[:, :T], [bb], T, self.pv[:, 1:2], [B("pv")], xb[0][:, :T], xb[1])
                self.rope_apply(xb[0][:, :T], xb[1], T, 0, self.kA[:, t0:t0 + T], B("kA"))
            for j in range(2):
                bank, bb = self.proj_fm(lay, f"BK{j}", T, self.rr("pj", 2))
                if kind == CTX:
                    k.op("act", I.activation(out=self.kB[:, j, t0:t0 + T], in_=bank[:, :T], func=AF.Copy), r=[bb], w=[B("kB")])
                else:
                    xb = self.PTs()
                    k.op("act", I.activation(out=xb[0][:, :T], in_=bank[:, :T], func=AF.Copy), r=[bb], w=[xb[1]])
                    self.rope_apply(xb[0][:, :T], xb[1], T, 1, self.kB[:, j, t0:t0 + T], B("kB"))
            for name, dst, dbuf in (("C0", self.uC[:, 0, t0:t0 + T], B("uC")), ("C1", self.uC[:, 1, t0:t0 + T], B("uC")),
                                    ("DK0", self.kD[:, 0, t0:t0 + T], B("kD")), ("DK1", self.kD[:, 1, t0:t0 + T], B("kD"))):
                bank, bb = self.proj_fm(lay, name, T, self.rr("pj", 2))
                k.op("act", I.activation(out=dst, in_=bank[:, :T], func=AF.Copy), r=[bb], w=[dbuf])
            self.pool_finish()
            if kind == CTX:
                self.pool_blocks([(0, 256, CTX)])
            else:
                l = (t0 - NCTX) // 512
                subs = ([2 * l - 1] if l >= 1 else []) + [2 * l] + ([7] if l == 3 else [])
                self.pool_blocks([(NCTX + 256 * s_, 256, LAT) for s_ in subs])
            for tl in range(T // 128):
                gt = t0 // 128 + tl
                cs = slice(tl * 128, (tl + 1) * 128)
                b2, bb2 = k.banks[2]
                b3, bb3 = k.banks[3]
                for kc in range(8):
                    k.op("pe", I.matmul(b2[:, 0:384], self.hm[:, kc, cs], self.wtbf[:, kc, 0:384], start=(kc == 0), stop=(kc == 7)),
                         r=[B("hm"), B("wtbf")], w=[bb2])
                for kc in range(8):
                    k.op("pe", I.matmul(b3[:, 0:512], self.hm[:, kc, cs], self.wtbf[:, kc, 384:896], start=(kc == 0), stop=(kc == 7)),
                         r=[B("hm"), B("wtbf")], w=[bb3])
                k.op("act", I.activation(out=self.vA[:, gt, 0, 0:64], in_=b2[:, 0:64], func=AF.Copy), r=[bb2], w=[B("vA")])
                k.op("act", I.activation(out=self.vA[:, gt, 1, 64:128], in_=b2[:, 64:128], func=AF.Copy), r=[bb2], w=[B("vA")])
                for h in range(4):
                    o0 = 0 if h % 2 == 0 else 64
                    k.op("dve" if h < 2 else "act",
                         (I.tensor_copy(out=self.vB[:, gt, h, o0:o0 + 64], in_=b2[:, 128 + 64 * h:192 + 64 * h])) if h < 2 else
                         (I.activation(out=self.vB[:, gt, h, o0:o0 + 64], in_=b2[:, 128 + 64 * h:192 + 64 * h], func=AF.Copy)),
                         r=[bb2], w=[B("vB")])
                k.op("dve", I.tensor_copy(out=self.vD[:, gt].rearrange("p h e -> p (h e)"), in_=b3[:, 256:512]), r=[bb3], w=[B("vD")])
                self.chk("KVc1")
                for d in range(2):
                    for h in range(4):
                        k.op("dve", I.tensor_scalar(out=self.ktmp[:, d, h, :], in0=b3[:, 64 * h:64 * h + 64], scalar1=self.tabk[:, d, h:h + 1], scalar2=None,
                                                    op0=ALU.mult),
                             r=[bb3, B("tabk")], w=[B("ktmp")])
                self.chk("KVc2")
                b7, bb7 = k.banks[7]
                for d in range(2):
                    for h in range(4):
                        j, par = h // 2, h % 2
                        pr = slice(par * 64, par * 64 + 64)
                        k.op("pe", I.matmul(b7[pr, d * 128 + j * 64:d * 128 + j * 64 + 64], self.ktmp[:, d, h, :],
                                                                                 self.vD[:, gt, h, :], start=True, stop=True),
                             r=[B("ktmp"), B("vD")], w=[bb7])
                self.chk("KVc3")
                k.op("dve", I.tensor_copy(out=self.SfB[:, gt], in_=self.S32[:, 0]), r=[B("S32")], w=[B("SfB")])
                for j in range(2):
                    k.op("dve", I.scalar_tensor_tensor(out=self.S32[:, 0, j, :], in0=self.S32[:, 0, j, :], scalar=self.g128[:, j:j + 1],
                                                                     in1=b7[:, j * 64:j * 64 + 64], op0=ALU.mult, op1=ALU.add),
                         r=[B("S32"), B("g128"), bb7], w=[B("S32")])
                k.op("dve", I.tensor_copy(out=self.KVB[:, gt].rearrange("p j e -> p (j e)"), in_=b7[:, 128:256]), r=[bb7], w=[B("KVB")])
            self.chk("KVc")
        self.chk("KVd")
        self.pool_finish()
        order = [1, 0] + list(range(NT - 1, 1, -1))
        for i, t in enumerate(order):
            k.op("pool", I.tensor_copy(out=self.SbB[:, t], in_=self.S32[:, 1]), r=[B("S32")], w=[B("SbB")])
            if i + 1 < len(order):
                for j in range(2):
                    k.op("dve", I.scalar_tensor_tensor(out=self.S32[:, 1, j, :], in0=self.S32[:, 1, j, :], scalar=self.g128[:, 2 + j:3 + j],
                                                                          in1=self.KVB[:, t, j, :], op0=ALU.mult, op1=ALU.add),
                         r=[B("S32"), B("g128"), B("KVB")], w=[B("S32")])

    def pool_blocks(self, pblocks):
        k = self.k
        B = k.buf
        for (t0g, T, kind) in pblocks:
            L = NCTX if kind == CTX else NLAT
            seq0 = 0 if kind == CTX else NCTX
            t0 = t0g - seq0
            invsrc = self.inp["invc_ctx" if kind == CTX else "invc_lat"]
            k.dma("sp", self.invb[:, :, :T], invsrc[:, :, t0:t0 + T], w=[B("invb")])
            W = T + 48
            ta, tb = max(0, t0 - 16), min(L, t0 + T + 16)
            qa, qb = ta - t0 + 32, tb - t0 + 32
            for c in range(2):
                pi = self.rr("poolb", 6)
                P_, A_, B_, C_ = [p_[:, 0:W] for p_ in self.ptmp]
                pb = [B("ptmp", i) for i in range(4)]
                k.op("pool", I.memset(P_, 0.0), w=[pb[0]])
                k.op("pool", I.tensor_copy(out=P_[:, qa:qb], in_=self.uC[:, c, seq0 + ta:seq0 + tb]), r=[B("uC")], w=[pb[0]])
                k.op("pool", I.tensor_tensor(out=A_[:, 16:W], in0=P_[:, 16:W], in1=P_[:, 15:W - 1], op=ALU.add), r=[pb[0]], w=[pb[1]])
                k.op("pool", I.tensor_tensor(out=B_[:, 16:W], in0=A_[:, 16:W], in1=A_[:, 14:W - 2], op=ALU.add), r=[pb[1]], w=[pb[2]])
                if c == 0:
                    lo, lo_b, lo_sh = A_, pb[1], 0
                    up, up_b, up_sh = B_, pb[2], 1
                else:
                    k.op("pool", I.tensor_tensor(out=C_[:, 16:W], in0=B_[:, 16:W], in1=B_[:, 12:W - 4], op=ALU.add), r=[pb[2]], w=[pb[3]])
                    k.op("pool", I.tensor_tensor(out=A_[:, 16:W], in0=C_[:, 16:W], in1=C_[:, 8:W - 8], op=ALU.add), r=[pb[3]], w=[pb[1]])
                    lo, lo_b, lo_sh = C_, pb[3], 3
                    up, up_b, up_sh = A_, pb[1], 7
                for (pr, src, sbuf_, sh) in ((slice(0, 64), lo, lo_b, lo_sh), (slice(64, 128), up, up_b, up_sh)):
                    k.op("pool", I.tensor_tensor(out=src[pr, 32 + sh:32 + sh + T], in0=src[pr, 32 + sh:32 + sh + T],
                                                                                      in1=self.invb[pr, c, :T], op=ALU.mult),
                         r=[sbuf_, B("invb")], w=[sbuf_])
                    k.op("pool", I.tensor_tensor(out=self.poolbs[pi][pr, :T], in0=src[pr, 32 + sh:32 + sh + T],
                                                                                        in1=P_[pr, 32:32 + T], op=ALU.subtract),
                         r=[sbuf_, pb[0]], w=[B("poolb", pi)])
                self.pool_pending.append((pi, c, t0g, T))

    def pool_finish(self):
        k = self.k
        B = k.buf
        b7, bb7 = k.banks[7]
        for (pi, c, t0g, T) in self.pool_pending:
            k.op("pe", I.matmul(b7[:, :T], self.pwb[:, c, :], self.poolbs[pi][:, :T], start=True, stop=True), r=[B("pwb"), B("poolb", pi)], w=[bb7])
            k.op("act", I.activation(out=self.yC[:, c, t0g:t0g + T], in_=b7[:, :T], func=AF.Identity, scale=self.pv[:, 4 + c:5 + c]),
                 r=[bb7, B("pv")], w=[B("yC")])
        self.pool_pending = []


    def attn_tiles(self, kind):
        return list(range(2)) if kind == CTX else list(range(NT))

    def stage_Y(self, lay, with_ctx):
        k = self.k
        B = k.buf
        self.stage = "Y"
        self.setlay(lay)
        mG = self.mG
        hbm, hbuf = self.ff, B("ff")
        wo = self.inp[f"wo{lay}"]
        wf = self.inp[f"winf{lay}"]
        self.plan_w([x for _ in self.blocks(with_ctx) for x in
                     [((("winf", lay), FM_IDX[n], 15), wf[FM_IDX[n]]) for n in ("AQ0", "AQ1", "BQ0", "BQ1", "DQ0", "DQ1", "G0", "G1")]
                     + [((("wo", lay), oc, 8), wo[oc]) for oc in range(8)]])
        for (t0, T, kind) in self.blocks(with_ctx):
            tiles = self.attn_tiles(kind)
            nch = T // 128
            k.dma("sp", hbm[:, :, :T], self.hs[:, :, t0:t0 + T].rearrange("c p t -> p c t"), r=[B("hs", t0)], w=[hbuf])
            self.modulate(hbm, hbuf, T, 1, kind)
            if kind == LAT:
                self.load_rope(t0, T)
            for j in range(2):
                bank, bb = self.proj_fm(lay, f"AQ{j}", T, self.rr("pj", 4))
                if kind == CTX:
                    self.headnorm(bank[:, :T], [bb], T, self.pv[:, 0:1], [B("pv")], self.qA[:, j, :T], B("qA"))
                else:
                    xb = self.PTs()
                    self.headnorm(bank[:, :T], [bb], T, self.pv[:, 0:1], [B("pv")], xb[0][:, :T], xb[1])
                    self.rope_apply(xb[0][:, :T], xb[1], T, 0, self.qA[:, j, :T], B("qA"))
            for j in range(2):
                bank, bb = self.proj_fm(lay, f"BQ{j}", T, self.rr("pj", 4))
                if kind == CTX:
                    k.op("act", I.activation(out=self.qB[:, j, :T], in_=bank[:, :T], func=AF.Copy), r=[bb], w=[B("qB")])
                else:
                    xb = self.PTs()
                    k.op("act", I.activation(out=xb[0][:, :T], in_=bank[:, :T], func=AF.Copy), r=[bb], w=[xb[1]])
                    self.rope_apply(xb[0][:, :T], xb[1], T, 1, self.qB[:, j, :T], B("qB"))
            for j in range(2):
                bank, bb = self.proj_fm(lay, f"DQ{j}", T, self.rr("pj", 4))
                k.op("act", I.activation(out=self.qD[:, j, :T], in_=bank[:, :T], func=AF.Copy), r=[bb], w=[B("qD")])
                for d in range(2):
                    for ci in range(nch):
                        k.op("dve", I.tensor_tensor(out=self.qfb[:, d, j, ci * 128:(ci + 1) * 128], in0=bank[:, ci * 128:(ci + 1) * 128],
                                                    in1=self.tabq[:, d, j, :], op=ALU.mult),
                             r=[bb, B("tabq")], w=[B("qfb")])
            for j in range(2):
                bank, bb = self.proj_fm(lay, f"G{j}", T, self.rr("pj", 4))
                k.op("act", I.activation(out=self.gate[:, j, :T], in_=bank[:, :T], func=AF.Silu), r=[bb], w=[B("gate")])
            for c in range(2):
                obs = [k.banks[4], k.banks[5]]
                pend = []
                for i, st in enumerate(tiles):
                    newp = []
                    for par in range(2):
                        pr = slice(par * 64, par * 64 + 64)
                        sbank, sbb = k.banks[SCB[self.rr("sc6", 6)]]
                        k.op("pe", I.matmul(sbank[:, :T], self.kA[pr, st * 128:(st + 1) * 128], self.qA[pr, c, :T], start=True, stop=True),
                             r=[B("kA"), B("qA")], w=[sbb])
                        newp.append((sbank, sbb, par))
                    if len(pend) >= 4:
                        pend.pop(0)()
                        pend.pop(0)()
                    for (sbank, sbb, par) in newp:
                        pt = self.PTs()
                        k.op("act", I.activation(out=pt[0][:, :T], in_=sbank[:, :T], func=AF.Exp, scale=0.125), r=[sbb], w=[pt[1]])

                        def pv_mm(st=st, pt=pt, i=i, par=par):
                            ob, obb = obs[par]
                            k.op("pe", I.matmul(ob[:, :T], self.vA[:, st, par, :], pt[0][:, :T], start=(i == 0), stop=(i == len(tiles) - 1)),
                                 r=[B("vA"), pt[1]], w=[obb])
                            self.warm()
                        pend.append(pv_mm)
                for f in pend:
                    f()
                for par in range(2):
                    pr = slice(par * 64, par * 64 + 64)
                    dn = slice(64, 128) if par == 0 else slice(0, 64)
                    ob, obb = obs[par]
                    nt, dt_ = self.fts(), self.fts()
                    k.op("act", I.activation(out=nt[0][pr, :T], in_=ob[pr, :T], func=AF.Copy), r=[obb], w=[nt[1]])
                    k.op("act", I.activation(out=dt_[0][pr, :T], in_=ob[dn, :T], func=AF.Copy), r=[obb], w=[dt_[1]])
                    k.op("dve", I.reciprocal(out=dt_[0][pr, :T], in_=dt_[0][pr, :T]), r=[dt_[1]], w=[dt_[1]])
                    k.op("dve", I.tensor_tensor(out=self.yblk[pr, c, :T], in0=nt[0][pr, :T], in1=dt_[0][pr, :T], op=ALU.mult),
                         r=[nt[1], dt_[1]], w=[B("yblk")])
            scB = 32 ** -0.5
            for j in range(2):
                yd = (self.ydt, B("ydt"))
                for par in range(2):
                    h = 2 * j + par
                    pr = slice(par * 64, par * 64 + 64)
                    dn = slice(64, 128) if par == 0 else slice(0, 64)
                    obs = [k.banks[4], k.banks[5]]
                    pend = []
                    for i, st in enumerate(tiles):
                        newp = []
                        for mp in range(2):
                            base = par * 64 + mp * 32
                            sbank, sbb = k.banks[SCB[self.rr("sc6", 6)]]
                            tp = (96, 0) if base == 96 else None
                            k.op("pe", I.matmul(
                                sbank[:, :T], self.kB[base:base + 32, j, st * 128:(st + 1) * 128], self.qB[base:base + 32, j, :T],
                                start=True, stop=True, tile_position=tp),
                                r=[B("kB"), B("qB")], w=[sbb])
                            newp.append((sbank, sbb, mp))
                        if len(pend) >= 4:
                            pend.pop(0)()
                            pend.pop(0)()
                        for (sbank, sbb, mp) in newp:
                            pt = self.PTs()
                            k.op("act", I.activation(out=pt[0][:, :T], in_=sbank[:, :T], func=AF.Exp, scale=scB), r=[sbb], w=[pt[1]])

                            def pv_mm(st=st, pt=pt, i=i, h=h, mp=mp):
                                ob, obb = obs[mp]
                                k.op("pe", I.matmul(ob[:, :T], self.vB[:, st, h, :], pt[0][:, :T], start=(i == 0), stop=(i == len(tiles) - 1)),
                                     r=[B("vB"), pt[1]], w=[obb])
                                self.warm()
                            pend.append(pv_mm)
                    for f in pend:
                        f()
                    n1, d1, n2, d2 = self.fts(), self.fts(), self.fts(), self.fts()
                    (o1, o1b), (o2, o2b) = obs
                    k.op("act", I.activation(out=n1[0][pr, :T], in_=o1[pr, :T], func=AF.Copy), r=[o1b], w=[n1[1]])
                    k.op("act", I.activation(out=d1[0][pr, :T], in_=o1[dn, :T], func=AF.Copy), r=[o1b], w=[d1[1]])
                    k.op("act", I.activation(out=n2[0][pr, :T], in_=o2[pr, :T], func=AF.Copy), r=[o2b], w=[n2[1]])
                    k.op("act", I.activation(out=d2[0][pr, :T], in_=o2[dn, :T], func=AF.Copy), r=[o2b], w=[d2[1]])
                    k.op("dve", I.reciprocal(out=d1[0][pr, :T], in_=d1[0][pr, :T]), r=[d1[1]], w=[d1[1]])
                    k.op("dve", I.reciprocal(out=d2[0][pr, :T], in_=d2[0][pr, :T]), r=[d2[1]], w=[d2[1]])
                    k.op("dve", I.tensor_tensor(out=n1[0][pr, :T], in0=n1[0][pr, :T], in1=d1[0][pr, :T], op=ALU.mult), r=[n1[1], d1[1]], w=[n1[1]])
                    k.op("dve", I.tensor_tensor(out=n2[0][pr, :T], in0=n2[0][pr, :T], in1=d2[0][pr, :T], op=ALU.mult), r=[n2[1], d2[1]], w=[n2[1]])
                    k.op("dve", I.scalar_tensor_tensor(out=yd[0][pr, :T], in0=n2[0][pr, :T], scalar=self.neglam[pr, 0:1],
                                                       in1=n1[0][pr, :T], op0=ALU.mult, op1=ALU.add),
                         r=[n2[1], n1[1], B("neglam")], w=[yd[1]])
                self.headnorm(yd[0][:, :T], [yd[1]], T, self.pv[:, 2:3], [B("pv")], self.yblk[:, 2 + j, :T], B("yblk"), imm=1.0 - self.lam_init)
            for j in range(2):
                oL, oLb = k.banks[4]
                oU, oUb = k.banks[5]
                for cidx in range(nch):
                    gt = t0 // 128 + cidx
                    cs = slice(cidx * 128, (cidx + 1) * 128)
                    for par in range(2):
                        h = 2 * j + par
                        pr = slice(par * 64, par * 64 + 64)
                        ob, obb = (oL, oLb) if par == 0 else (oU, oUb)
                        sb_i = self.rr("sc", 4)
                        sbank, sbb = k.banks[sb_i]
                        k.op("pe", I.matmul(sbank[:, 0:128], self.kD[pr, j, gt * 128:(gt + 1) * 128],
                                                                                           self.qD[pr, j, cs], start=True, stop=True),
                             r=[B("kD"), B("qD")], w=[sbb])
                        pt = self.PTs()
                        k.op("dve", I.tensor_tensor(out=pt[0][:, 0:128], in0=sbank[:, 0:128], in1=self.mask[:, h, :], op=ALU.mult),
                             r=[sbb, B("mask")], w=[pt[1]])
                        k.op("pe", I.matmul(ob[pr, cs], self.vD[:, gt, h, :], pt[0][:, 0:128], start=True, stop=False),
                             r=[B("vD"), pt[1]], w=[obb])
                        k.op("pe", I.matmul(ob[pr, cs], self.SfB[pr, gt, j, :], self.qfb[pr, 0, j, cs], start=False, stop=False),
                             r=[B("SfB"), B("qfb")], w=[obb])
                        k.op("pe", I.matmul(ob[pr, cs], self.SbB[pr, gt, j, :], self.qfb[pr, 1, j, cs], start=False, stop=True),
                             r=[B("SbB"), B("qfb")], w=[obb])
                od = self.fts()
                k.op("act", I.activation(out=od[0][0:64, :T], in_=oL[0:64, :T], func=AF.Copy), r=[oLb], w=[od[1]])
                k.op("act", I.activation(out=od[0][64:128, :T], in_=oU[64:128, :T], func=AF.Copy), r=[oUb], w=[od[1]])
                on = self.fts()
                self.headnorm(od[0][:, :T], [od[1]], T, self.pv[:, 3:4], [B("pv")], on[0][:, :T], on[1])
                k.op("dve", I.tensor_tensor(out=self.yblk[:, 4 + j, :T], in0=on[0][:, :T], in1=self.gate[:, j, :T], op=ALU.mult),
                     r=[on[1], B("gate")], w=[B("yblk")])
            if getattr(self, "debug_y", False):
                if "dbg_y" not in self.__dict__:
                    self.dbg_y = k.dram("dbg_y", [6, 128, NTOK], BF16, "ExternalOutput")
                k.dma("sp", self.dbg_y[:, :, t0:t0 + T].rearrange("c p t -> p c t"), self.yblk[:, :, :T], r=[B("yblk")], w=[B("dbg_y")])
            ysrc = [(self.yblk[:, 0, :T], B("yblk")), (self.yblk[:, 1, :T], B("yblk")), (self.yblk[:, 2, :T], B("yblk")), (self.yblk[:, 3, :T], B("yblk")),
                    (self.yC[:, 0, t0:t0 + T], B("yC")), (self.yC[:, 1, t0:t0 + T], B("yC")), (self.yblk[:, 4, :T], B("yblk")), (self.yblk[:, 5, :T], B("yblk"))]
            for oc in range(8):
                s = self.load_w(wo[oc])
                bank, bb = k.banks[self.rr("sc", 4)]
                for kc in range(8):
                    yap, ybuf = ysrc[kc]
                    k.op("pe", I.matmul(bank[:, :T], self.wbf[s][:, kc, :], yap, start=(kc == 0), stop=(kc == 7)),
                         r=[B("wbf", s), ybuf], w=[bb])
                k.op("act", I.activation(out=self.ff[:, oc, :T], in_=bank[:, :T], func=AF.Copy), r=[bb], w=[B("ff")])
            self.rms_rstd(self.ff[:, :, :T], B("ff"), T)
            for kc in range(8):
                s = self.rr("hc", 2)
                hcb = B("hc", s)
                k.dma("sp", self.hc[s][:, :T], self.hs[kc, :, t0:t0 + T], r=[B("hs", t0)], w=[hcb])
                f = self.fts()
                k.op("dve", I.scalar_tensor_tensor(out=f[0][:, :T], in0=self.ff[:, kc, :T], scalar=mG[:, 1, kc, kind:kind + 1],
                                                                        in1=self.rstd[:, :T], op0=ALU.mult, op1=ALU.mult),
                     r=[B("ff"), self.mB("mG"), B("rstd")], w=[f[1]])
                k.op("pool", I.tensor_tensor(out=self.hc[s][:, :T], in0=self.hc[s][:, :T], in1=f[0][:, :T], op=ALU.add),
                     r=[f[1], hcb], w=[hcb])
                k.dma("sp", self.hs[kc, :, t0:t0 + T], self.hc[s][:, :T], r=[hcb], w=[B("hs2", t0)], sembuf=hcb)
        for (t0, T, kind) in self.blocks(with_ctx):
            B("hs", t0).writers.update(B("hs2", t0).writers)

    def dump(self, name, ap, shape, bufs, dt=F32):
        k = self.k
        d = k.dram("dbg_" + name, shape, dt, "ExternalOutput")
        k.dma("sp", d, ap, r=bufs, w=[k.buf("dbg_" + name)])

    def dump_hs(self):
        k = self.k
        d = k.dram("dbg_hs", [8, 128, NTOK], F32, "ExternalOutput")
        k.dma("sp", d, self.hs, r=[k.buf("hs", t0) for (t0, T, kind) in self.blocks(True)], w=[k.buf("dbg_hs")])

    def dump_bf(self, name, ap, shape, bufs):
        k = self.k
        raise NotImplementedError

    def build(self, stop_after=None):
        k = self.k
        self.stage = "F"
        self.prologue()
        self.mod(0)
        if stop_after == "MOD":
            self.dump("modT", self.modT[:].rearrange("p a b c d -> p (a b c d)"), [128, 144], [k.buf("modT", 0)])
            return k.emit()
        self.modgen = self.mod_gen(1, own=True)
        self.stage_F(self.inp["xT"], pre_of=0)
        while self.mod_tick():
            pass
        if stop_after == "F0":
            self.dump_hs()
            return k.emit()
        for lay in range(DEPTH):
            last = lay == DEPTH - 1
            k.barrier()
            self.prep_mixer(lay)
            if stop_after == f"PM{lay}":
                self.dump("mask", self.mask[:].rearrange("p a b -> p (a b)"), [128, 512], [k.buf("mask")])
                self.dump("tabq", self.tabq[:].rearrange("p a b c -> p (a b c)"), [128, 512], [k.buf("tabq")])
                self.dump("tabk", self.tabk[:].rearrange("p a b -> p (a b)"), [128, 8], [k.buf("tabk")])
                self.dump("neglam", self.neglam[:], [128, 1], [k.buf("neglam")])
                self.dump("g128", self.g128[:], [128, 4], [k.buf("g128")])
                return k.emit()
            self.kv_stop = stop_after
            try:
                self.stage_KV(lay)
            except _Stop:
                return k.emit()
            if stop_after == f"KV{lay}":
                return k.emit()
            k.barrier()
            self.stage_Y(lay, with_ctx=not last)
            if stop_after == f"Y{lay}":
                self.dump_hs()
                if getattr(self, "debug_y", False):
                    self.dump("yC", self.yC, [128, 2, NTOK], [k.buf("yC")], dt=BF16)
                return k.emit()
            k.barrier()
            self.stage = "F"
            if not last:
                self.stage_F(self.hs, post_of=lay, pre_of=lay + 1, with_ctx=True)
            else:
                self.stage_F(self.hs, post_of=lay, with_ctx=False, dst=self.out)
            if stop_after == f"F{lay + 1}":
                self.dump_hs()
                return k.emit()
        return k.emit()


def _shapes(sh, per0):
    shapes = {}
    for name, a in list(sh.items()) + list(per0.items()):
        shapes[name] = (a.shape, F32)
    return shapes


def kernel(**inputs):
    inputs = {k_: np.asarray(v) for k_, v in inputs.items()}
    sh, per = host_prep(inputs)
    prog = Prog(_shapes(sh, per[0]))
    nc = prog.build()
    in_maps = [{k_: v for k_, v in dict(sh, **p).items() if k_ in prog.inp} for p in per]
    res = run_bass_kernel_spmd(nc, in_maps, core_ids=list(range(8)))
    outs = []
    for r in res.results:
        o = np.asarray(r["out"]).reshape(D, NLAT)
        outs.append(o.T)
    return np.ascontiguousarray(np.stack(outs, axis=0)).astype(np.float32)
```
